# Optimizing a Trainium2 kernel written in Bass

```python
import math
import jax, jax.numpy as jnp
from jax import lax
import numpy as np

D_MODEL = 2048
BATCH = 1
SEQ = 8192
DEPTH = 1
DEC_BATCH = 2
DEC_SEQ = 8192
PAST_LEN = 128

MIX_WIDTH = D_MODEL
RET_WIDTH = MIX_WIDTH // 2
CONV_WIDTH = MIX_WIDTH - RET_WIDTH
RET_HEADS = 8
RET_HEAD_DIM = RET_WIDTH // RET_HEADS
CONV_GROUPS = 8
CHUNK = 128
ROPE_BASE = 10000.0
CONV_K = 3
IN_PROJ_WIDTH = 4 * RET_WIDTH + 3 * CONV_WIDTH

PEER_HEADS = 8
PEER_N_KEYS = 128
PEER_N_EXPERTS = PEER_N_KEYS * PEER_N_KEYS
PEER_QUERY_DIM = 256
PEER_HALF = PEER_QUERY_DIM // 2
PEER_TOPK = 16
TOKEN_BLOCK = 128

LN_EPS = 1e-5
GN_EPS = 1e-6
DEEPNORM_ALPHA = (2.0 * DEPTH) ** 0.25
DEEPNORM_BETA = (8.0 * DEPTH) ** -0.25

kernel_name = "hymba_retention_shortconv_peer_encoder"


def layer_norm(x, g, b):
    xf = x.astype(jnp.float32)
    mu = jnp.mean(xf, axis=-1, keepdims=True)
    var = jnp.mean(jnp.square(xf - mu), axis=-1, keepdims=True)
    y = (xf - mu) * lax.rsqrt(var + LN_EPS) * g.astype(jnp.float32) + b.astype(jnp.float32)
    return y.astype(x.dtype)


def rotary(x):
    s, dh = x.shape[1], x.shape[-1]
    inv = ROPE_BASE ** (-jnp.arange(0, dh, 2, dtype=jnp.float32) / dh)
    ang = jnp.arange(s, dtype=jnp.float32)[:, None] * inv[None, :]
    cos = jnp.cos(ang)[None, :, None, :]
    sin = jnp.sin(ang)[None, :, None, :]
    x1, x2 = x[..., : dh // 2], x[..., dh // 2:]
    return jnp.concatenate([x1 * cos - x2 * sin, x1 * sin + x2 * cos], axis=-1)


def retention_chunkwise(q, k, v, log_rate):
    b, s, h, dh = q.shape
    n_chunks = s // CHUNK
    log_gamma = -jnp.exp(log_rate.astype(jnp.float32))
    j = jnp.arange(CHUNK, dtype=jnp.float32)
    diff = j[:, None] - j[None, :]
    d_intra = jnp.where(diff[None] >= 0,
                        jnp.exp(log_gamma[:, None, None] * jnp.maximum(diff, 0.0)[None]), 0.0)
    xi = jnp.exp(log_gamma[:, None] * (j + 1.0)[None])
    zeta = jnp.exp(log_gamma[:, None] * (CHUNK - 1.0 - j)[None])
    gamma_chunk = jnp.exp(log_gamma * CHUNK)

    def to_chunks(t):
        return t.reshape(b, n_chunks, CHUNK, h, dh).transpose(1, 0, 3, 2, 4)

    def step(state, inp):
        qc, kc, vc = inp
        scores = jnp.einsum('bhid,bhjd->bhij', qc, kc) * d_intra[None]
        inner = jnp.einsum('bhij,bhjd->bhid', scores, vc)
        cross = jnp.einsum('bhid,bhde->bhie', qc, state) * xi[None, :, :, None]
        new_state = state * gamma_chunk[None, :, None, None] + jnp.einsum(
            'bhjd,bhje->bhde', kc * zeta[None, :, :, None], vc)
        return new_state, inner + cross

    init = jnp.zeros((b, h, dh, dh), jnp.float32)
    _, out = lax.scan(step, init, (to_chunks(q), to_chunks(k), to_chunks(v)))
    return out.transpose(1, 0, 3, 2, 4).reshape(b, s, h, dh)


def mixer(x, w_in, rate_f, rate_b, gn_g, conv_w, w_out):
    b, s, _ = x.shape
    proj = x @ w_in
    r = RET_WIDTH
    c = CONV_WIDTH
    q, k, v, g, hc, bg, cg = jnp.split(
        proj, [r, 2 * r, 3 * r, 4 * r, 4 * r + c, 4 * r + 2 * c], axis=-1)
    shp = (b, s, RET_HEADS, RET_HEAD_DIM)
    qf = rotary(q.reshape(shp).astype(jnp.float32))
    kf = rotary(k.reshape(shp).astype(jnp.float32)) * (RET_HEAD_DIM ** -0.5)
    vf = v.reshape(shp).astype(jnp.float32)
    o_fwd = retention_chunkwise(qf, kf, vf, rate_f)
    o_bwd = jnp.flip(retention_chunkwise(jnp.flip(qf, 1), jnp.flip(kf, 1), jnp.flip(vf, 1), rate_b), 1)
    o = o_fwd + o_bwd
    mu = jnp.mean(o, axis=-1, keepdims=True)
    var = jnp.mean(jnp.square(o - mu), axis=-1, keepdims=True)
    o = ((o - mu) * lax.rsqrt(var + GN_EPS)).reshape(b, s, r) * gn_g.astype(jnp.float32)
    ret_out = (o * jax.nn.silu(g.astype(jnp.float32))).astype(x.dtype)
    u = cg * hc
    up = jnp.pad(u, ((0, 0), (1, 1), (0, 0)))
    conv = up[:, :-2] * conv_w[0] + up[:, 1:-1] * conv_w[1] + up[:, 2:] * conv_w[2]
    conv_out = bg * conv
    return jnp.concatenate([ret_out, conv_out], axis=-1) @ w_out


def peer(x, w_query, keys_1, keys_2, exp_u, exp_v):
    b, s, d = x.shape
    xb = x.reshape((b * s) // TOKEN_BLOCK, TOKEN_BLOCK, d)

    def block(xt):
        q = (xt @ w_query).reshape(TOKEN_BLOCK, PEER_HEADS, PEER_QUERY_DIM)
        q1, q2 = q[..., :PEER_HALF], q[..., PEER_HALF:]
        s1 = jnp.einsum('thd,hnd->thn', q1, keys_1)
        s2 = jnp.einsum('thd,hnd->thn', q2, keys_2)
        v1, i1 = lax.top_k(s1, PEER_TOPK)
        v2, i2 = lax.top_k(s2, PEER_TOPK)
        cand = (v1[..., :, None] + v2[..., None, :]).reshape(TOKEN_BLOCK, PEER_HEADS, PEER_TOPK * PEER_TOPK)
        cand_idx = (i1[..., :, None] * PEER_N_KEYS + i2[..., None, :]).reshape(
            TOKEN_BLOCK, PEER_HEADS, PEER_TOPK * PEER_TOPK)
        best, pos = lax.top_k(cand, PEER_TOPK)
        eid = jnp.take_along_axis(cand_idx, pos, axis=-1)
        gate = jax.nn.softmax(best.astype(jnp.float32), axis=-1).astype(xt.dtype)
        u_sel = exp_u[eid]
        act = jax.nn.gelu(jnp.einsum('thkd,td->thk', u_sel, xt), approximate=False)
        return jnp.einsum('thk,thkd->td', gate * act, exp_v[eid])

    return lax.map(block, xb).reshape(b, s, d)


def encoder_layer(x, w_in, rate_f, rate_b, gn_g, conv_w, w_out, ln1_g, ln1_b,
                  w_query, keys_1, keys_2, exp_u, exp_v, ln2_g, ln2_b):
    h = layer_norm(DEEPNORM_ALPHA * x + mixer(x, w_in, rate_f, rate_b, gn_g, conv_w, w_out), ln1_g, ln1_b)
    return layer_norm(DEEPNORM_ALPHA * h + peer(h, w_query, keys_1, keys_2, exp_u, exp_v), ln2_g, ln2_b)


def setup_inputs(seed: int = 0) -> dict:
    key = jax.random.key(seed)
    ks = jax.random.split(key, 20)
    f32 = jnp.float32
    base_rate = jnp.log(2.0 ** (-5.0 - jnp.arange(RET_HEADS, dtype=f32)))
    return {
        "x_prompt": jax.random.normal(ks[0], (BATCH, SEQ, D_MODEL), f32),
        "x_sample": jax.random.normal(ks[1], (DEC_BATCH, DEC_SEQ, D_MODEL), f32),
        "w_in": jax.random.normal(ks[2], (DEPTH, D_MODEL, IN_PROJ_WIDTH), f32) * D_MODEL ** -0.5,
        "ret_log_rate_fwd": base_rate[None] + 0.1 * jax.random.normal(ks[3], (DEPTH, RET_HEADS), f32),
        "ret_log_rate_bwd": base_rate[None] + 0.1 * jax.random.normal(ks[4], (DEPTH, RET_HEADS), f32),
        "ret_gn_g": 1.0 + 0.02 * jax.random.normal(ks[5], (DEPTH, RET_WIDTH), f32),
        "conv_w": jax.random.normal(ks[6], (DEPTH, CONV_K, CONV_WIDTH), f32) * CONV_K ** -0.5,
        "w_out": jax.random.normal(ks[7], (DEPTH, MIX_WIDTH, D_MODEL), f32) * (MIX_WIDTH ** -0.5) * DEEPNORM_BETA,
        "ln1_g": 1.0 + 0.02 * jax.random.normal(ks[8], (DEPTH, D_MODEL), f32),
        "ln1_b": 0.02 * jax.random.normal(ks[9], (DEPTH, D_MODEL), f32),
        "peer_w_query": jax.random.normal(ks[10], (DEPTH, D_MODEL, PEER_HEADS * PEER_QUERY_DIM), f32) * D_MODEL ** -0.5,
        "peer_keys_1": jax.random.normal(ks[11], (DEPTH, PEER_HEADS, PEER_N_KEYS, PEER_HALF), f32) * PEER_HALF ** -0.5,
        "peer_keys_2": jax.random.normal(ks[12], (DEPTH, PEER_HEADS, PEER_N_KEYS, PEER_HALF), f32) * PEER_HALF ** -0.5,
        "peer_u": jax.random.normal(ks[13], (DEPTH, PEER_N_EXPERTS, D_MODEL), f32) * D_MODEL ** -0.5,
        "peer_v": jax.random.normal(ks[14], (DEPTH, PEER_N_EXPERTS, D_MODEL), f32) * (PEER_HEADS ** -0.5) * DEEPNORM_BETA,
        "ln2_g": 1.0 + 0.02 * jax.random.normal(ks[15], (DEPTH, D_MODEL), f32),
        "ln2_b": 0.02 * jax.random.normal(ks[16], (DEPTH, D_MODEL), f32),
    }


def reference(x_prompt, x_sample, w_in, ret_log_rate_fwd, ret_log_rate_bwd, ret_gn_g, conv_w, w_out,
              ln1_g, ln1_b, peer_w_query, peer_keys_1, peer_keys_2, peer_u, peer_v, ln2_g, ln2_b):
    y_prompt = x_prompt
    y_sample = x_sample
    for l in range(DEPTH):
        params = (w_in[l], ret_log_rate_fwd[l], ret_log_rate_bwd[l], ret_gn_g[l], conv_w[l], w_out[l],
                  ln1_g[l], ln1_b[l], peer_w_query[l], peer_keys_1[l], peer_keys_2[l],
                  peer_u[l], peer_v[l], ln2_g[l], ln2_b[l])
        y_prompt = encoder_layer(y_prompt, *params)
        y_sample = encoder_layer(y_sample, *params)
    return (y_prompt, y_sample)
```

```python
import numpy as np
from contextlib import ExitStack
import concourse.bass as bass
import concourse.mybir as mybir
from concourse.bass_utils import run_bass_kernel_spmd

F32 = mybir.dt.float32
BF16 = mybir.dt.bfloat16
I32 = mybir.dt.int32
U32 = mybir.dt.uint32
AF = mybir.ActivationFunctionType
ALU = mybir.AluOpType
AX = mybir.AxisListType

ALPHA = 2.0 ** 0.25
LN_EPS = 1e-5
GN_EPS = 1e-6
SK = 128.0 ** -0.5
NSEG = 3
NBLK = 7
NCH = 8

C_RATE = 0
C_EPF, C_MPF, C_EPB, C_MPB = 16, 144, 272, 400
C_EXF, C_EXB = 528, 656
C_EZF1, C_EZB1 = 784, 785
C_EZFA, C_EZBA = 786, 794
C_EWF, C_MWF, C_EWB, C_MWB = 802, 810, 818, 826
C_IOTA = 834
C_ID = 850
NT = 978


class Buf:
    def __init__(self, t):
        self.t = t
        self.w = {}
        self.r = {}
        self.dkey = None
        self.dcnt = 0
        self.psum = False

    def __getitem__(self, k):
        return self.t[k]


class Prog:
    def __init__(self, nc, es):
        self.nc = nc
        self.es = es
        self.names = ['pe', 'dve', 'act', 'pool', 'sp']
        self.semh = {}
        for k in self.names:
            self.semh[k] = es.enter_context(nc.semaphore('s_' + k))
        self.cnt = {k: 0 for k in self.names}
        self.waited = {k: {} for k in self.names}
        self.q = {k: [] for k in self.names}
        self.nd = 0
        self.dbufs = []

    def _need(self, reads, writes, e=None):
        need = {}
        for b in reads:
            for k, v in b.w.items():
                if need.get(k, 0) < v:
                    need[k] = v
            if b.psum:
                for k, v in b.r.items():
                    if k != e and need.get(k, 0) < v:
                        need[k] = v
        for b in writes:
            for k, v in b.w.items():
                if need.get(k, 0) < v:
                    need[k] = v
            for k, v in b.r.items():
                if need.get(k, 0) < v:
                    need[k] = v
        return need

    def _waits(self, e, need):
        waits = []
        wd = self.waited[e]
        for k, v in need.items():
            if k == e and e == 'pe':
                continue
            if wd.get(k, 0) < v:
                wd[k] = v
                waits.append((self.semh[k], v))
        return waits

    def op(self, e, name, R=(), W=(), **kw):
        waits = self._waits(e, self._need(R, W, e))
        self.cnt[e] += 1
        tok = self.cnt[e]
        self.q[e].append((waits, name, kw, self.semh[e], 1))
        for b in R:
            if b.r.get(e, 0) < tok:
                b.r[e] = tok
        for b in W:
            b.w[e] = tok

    def dma(self, e, sb, out, in_, R=(), W=(), name='dma_start', nodep=False, **kw):
        if sb.dkey is None:
            sb.dkey = 'd%d' % self.nd
            self.nd += 1
            self.semh[sb.dkey] = self.es.enter_context(self.nc.semaphore(sb.dkey))
            self.dbufs.append(sb)
        waits = [] if nodep else self._waits(e, self._need(R, W))
        sb.dcnt += 16
        kw = dict(kw)
        kw['out'] = out
        kw['in_'] = in_
        self.q[e].append((waits, name, kw, self.semh[sb.dkey], 16))
        for b in R:
            if b.r.get(sb.dkey, 0) < sb.dcnt:
                b.r[sb.dkey] = sb.dcnt
        for b in W:
            b.w[sb.dkey] = sb.dcnt

    def barrier(self):
        need = {k: self.cnt[k] for k in self.names if self.cnt[k] > 0}
        for b in self.dbufs:
            need[b.dkey] = b.dcnt
        for e in self.names:
            waits = self._waits(e, need)
            if waits:
                self.q[e].append((waits, None, None, None, 0))

    def replay(self, e, E):
        for waits, name, kw, sem, inc in self.q[e]:
            for s, v in waits:
                E.wait_ge(s, v)
            if name is None:
                continue
            ins = getattr(E, name)(**kw)
            ins.then_inc(sem, inc)


def build(dbg=None):
    nc = bass.Bass("TRN2", target_bir_lowering=False)
    dbg = dbg or {}
    A_BLOCKS = dbg.get('a_blocks', [(s_, r_) for s_ in range(NSEG) for r_ in range(NBLK)])
    SEGS = dbg.get('segs', list(range(NSEG)))
    HEADS = dbg.get('heads', list(range(8)))
    GROUPS = dbg.get('groups', list(range(8)))
    TILES = dbg.get('tiles', list(range(NCH)))
    NTOK = dbg.get('ntok', 128)
    HST = dbg.get('hstage', 99)
    NUV = 6
    dumps = []

    def din(name, shape, dt=F32):
        return nc.dram_tensor(name, shape, dt, kind="ExternalInput").ap()

    xtok = din("xtok", [NSEG, 1024, 2048])
    xTs = din("xTs", [NSEG, 2048, 1024])
    xTh = din("xTh", [NSEG, 2048, 64])
    xTall = din("xTall", [NSEG, 2048, 1024 * NBLK])
    w_in = din("w_in", [2048, 7168])
    w_out = din("w_out", [2048, 2048])
    w_q = din("w_q", [2048, 2048])
    keysT = din("keysT", [128, 16, 128])
    pu = din("pu", [16384, 2048])
    pv = din("pv", [16384, 2048])
    ctab = din("ctab", [128, NT])
    csall = din("csall", [128, 2, 8 * NBLK, 64])
    csown = din("csown", [128, 2, 8, 64])
    gng = din("gng", [128, 1024])
    cwT = din("cwT", [128, 8, 3])
    lnv = din("lnv", [4, 128, 2048])
    y = nc.dram_tensor("y", [NSEG, 1024, 2048], F32, kind="ExternalOutput").ap()
    sin_d = nc.dram_tensor("sin_d", [NSEG, 2, 128, 1024], F32, kind=("ExternalInput" if dbg.get('sin_in') else "Internal")).ap()
    cat_in = din("cat_in", [128, 16, 1024]) if dbg.get('cat_in') else None
    uv16 = nc.dram_tensor("uv16", [16384, 4096], BF16, kind="Internal").ap()
    w_in16 = nc.dram_tensor("w_in16", [2048, 7168], BF16, kind="Internal").ap()
    w_out16 = nc.dram_tensor("w_out16", [2048, 2048], BF16, kind="Internal").ap()
    w_q16 = nc.dram_tensor("w_q16", [2048, 2048], BF16, kind="Internal").ap()

    es = ExitStack()
    with es:
        P = Prog(nc, es)

        uid = [0]

        def SB(name, shape, dt=F32, stack=es):
            uid[0] += 1
            return Buf(stack.enter_context(nc.sbuf_tensor("%s_%d" % (name, uid[0]), shape, dt)))

        def PS(name, shape, dt=F32, stack=es):
            uid[0] += 1
            b = Buf(stack.enter_context(nc.psum_tensor("%s_%d" % (name, uid[0]), shape, dt)))
            b.psum = True
            return b

        def dump(name, buf, ap, shape, dt=F32):
            if name not in dbg.get('dump', []):
                return
            d = nc.dram_tensor("dbg_" + name, shape, dt, kind="ExternalOutput").ap()
            P.dma('sp', buf, d, ap, R=[buf])

        ct = SB("ct", [128, NT])
        idb = SB("idb", [128, 128], BF16)
        lg = SB("lg", [128, 16])
        DmT = SB("DmT", [128, 8, 128])
        XiF = SB("XiF", [128, 8, 128])
        XiB = SB("XiB", [128, 8, 128])
        Z1 = SB("Z1", [128, 16])
        ZA = SB("ZA", [128, 2, 8, 8])
        G128 = SB("G128", [128, 16])
        Wc = SB("Wc", [128, 2, 8, 8])
        cso = SB("cso", [128, 2, 8, 64])
        gn = SB("gn", [128, 1024])
        cw = SB("cw", [128, 8, 3])
        kT16 = SB("kT16", [128, 16, 128], BF16)
        tA = SB("tA", [128, 128])
        tB = SB("tB", [128, 128])

        P.dma('sp', ct, ct[:], ctab, W=[ct])
        P.dma('pool', idb, idb[:], ctab[:, C_ID:C_ID + 128], W=[idb])
        P.dma('sp', cso, cso[:], csown, W=[cso])
        P.dma('sp', gn, gn[:], gng, W=[gn])
        P.dma('sp', cw, cw[:], cwT, W=[cw])
        P.dma('pool', kT16, kT16[:], keysT, W=[kT16])

        P.op('act', 'activation', R=[ct], W=[lg], out=lg[:], in_=ct[:, C_RATE:C_RATE + 16], func=AF.Exp)
        P.op('dve', 'tensor_scalar', R=[lg], W=[lg], out=lg[:], in0=lg[:], scalar1=-1.0, scalar2=None, op0=ALU.mult)
        for h in range(8):
            lf = lg[:, h:h + 1]
            lb = lg[:, 8 + h:9 + h]
            P.op('act', 'activation', R=[ct, lg], W=[tA], out=tA[:], in_=ct[:, C_EPF:C_EPF + 128], func=AF.Exp, scale=lf)
            P.op('dve', 'tensor_tensor', R=[tA, ct], W=[tA], out=tA[:], in0=tA[:], in1=ct[:, C_MPF:C_MPF + 128], op=ALU.mult)
            P.op('act', 'activation', R=[ct, lg], W=[tB], out=tB[:], in_=ct[:, C_EPB:C_EPB + 128], func=AF.Exp, scale=lb)
            P.op('dve', 'tensor_tensor', R=[tB, ct], W=[tB], out=tB[:], in0=tB[:], in1=ct[:, C_MPB:C_MPB + 128], op=ALU.mult)
            P.op('dve', 'tensor_tensor', R=[tA, tB], W=[DmT], out=DmT[:, h, :], in0=tA[:], in1=tB[:], op=ALU.add)
            P.op('act', 'activation', R=[ct, lg], W=[XiF], out=XiF[:, h, :], in_=ct[:, C_EXF:C_EXF + 128], func=AF.Exp, scale=lf)
            P.op('act', 'activation', R=[ct, lg], W=[XiB], out=XiB[:, h, :], in_=ct[:, C_EXB:C_EXB + 128], func=AF.Exp, scale=lb)
            P.op('act', 'activation', R=[ct, lg], W=[ZA], out=ZA[:, 0, :, h], in_=ct[:, C_EZFA:C_EZFA + 8], func=AF.Exp, scale=lf)
            P.op('act', 'activation', R=[ct, lg], W=[ZA], out=ZA[:, 1, :, h], in_=ct[:, C_EZBA:C_EZBA + 8], func=AF.Exp, scale=lb)
            P.op('act', 'activation', R=[ct, lg], W=[Wc], out=Wc[:, 0, :, h], in_=ct[:, C_EWF:C_EWF + 8], func=AF.Exp, scale=lf)
            P.op('act', 'activation', R=[ct, lg], W=[Wc], out=Wc[:, 1, :, h], in_=ct[:, C_EWB:C_EWB + 8], func=AF.Exp, scale=lb)
        P.op('dve', 'tensor_scalar', R=[DmT], W=[DmT], out=DmT[:], in0=DmT[:], scalar1=SK, scalar2=None, op0=ALU.mult)
        P.op('dve', 'tensor_scalar', R=[ZA], W=[ZA], out=ZA[:], in0=ZA[:], scalar1=SK, scalar2=None, op0=ALU.mult)
        P.op('dve', 'tensor_tensor', R=[Wc, ct], W=[Wc], out=Wc[:, 0], in0=Wc[:, 0],
             in1=ct[:, C_MWF:C_MWF + 8].unsqueeze(2).to_broadcast([128, 8, 8]), op=ALU.mult)
        P.op('dve', 'tensor_tensor', R=[Wc, ct], W=[Wc], out=Wc[:, 1], in0=Wc[:, 1],
             in1=ct[:, C_MWB:C_MWB + 8].unsqueeze(2).to_broadcast([128, 8, 8]), op=ALU.mult)
        P.op('dve', 'tensor_scalar', R=[lg, ct], W=[Z1], out=Z1[:, 0:8], in0=lg[:, 0:8], scalar1=ct[:, C_EZF1:C_EZF1 + 1], scalar2=None, op0=ALU.mult)
        P.op('dve', 'tensor_scalar', R=[lg, ct], W=[Z1], out=Z1[:, 8:16], in0=lg[:, 8:16], scalar1=ct[:, C_EZB1:C_EZB1 + 1], scalar2=None, op0=ALU.mult)
        P.op('act', 'activation', R=[Z1], W=[Z1], out=Z1[:], in_=Z1[:], func=AF.Exp)
        P.op('dve', 'tensor_scalar', R=[Z1], W=[Z1], out=Z1[:], in0=Z1[:], scalar1=SK, scalar2=None, op0=ALU.mult)
        P.op('act', 'activation', R=[lg], W=[G128], out=G128[:], in_=lg[:], func=AF.Exp, scale=128.0)

        dump('DmT', DmT, DmT[:], [128, 8, 128])
        dump('XiF', XiF, XiF[:], [128, 8, 128])
        dump('XiB', XiB, XiB[:], [128, 8, 128])
        dump('Z1', Z1, Z1[:], [128, 16])
        dump('ZA', ZA, ZA[:], [128, 2, 8, 8])
        dump('Wc', Wc, Wc[:], [128, 2, 8, 8])
        dump('G128', G128, G128[:], [128, 16])

        UV16 = Buf(uv16)
        CONV_ROWS = 256
        W16 = Buf(w_in16)
        conv_jobs = [(W16, w_in16, w_in, i, 128) for i in range(16)] + [(W16, w_out16, w_out, i, 256) for i in range(8)] + \
                    [(W16, w_q16, w_q, i, 256) for i in range(8)] + \
                    [(UV16, uv16[:, 0:2048], pu, i, CONV_ROWS) for i in range(16384 // CONV_ROWS)] + \
                    [(UV16, uv16[:, 2048:4096], pv, i, CONV_ROWS) for i in range(16384 // CONV_ROWS)]
        conv_jobs.reverse()

        def conv_step(nmax=1):
            for _ in range(nmax):
                if not conv_jobs:
                    return
                B_, dst_, src_, i_, nr = conv_jobs.pop()
                P.dma('pool', B_, dst_[i_ * nr:(i_ + 1) * nr, :], src_[i_ * nr:(i_ + 1) * nr, :], W=[B_], nodep=True)

        def rotary(src, dst, cos, sin, H, t1, t2, t3, t4, srcB, dstB, csB):
            cb = cos.unsqueeze(1).to_broadcast([128, H, 64])
            sb_ = sin.unsqueeze(1).to_broadcast([128, H, 64])
            x1 = src[:, :, 0:64]
            x2 = src[:, :, 64:128]
            P.op('dve', 'tensor_tensor', R=[srcB, csB], W=[t1], out=t1[:, 0:H, :], in0=x1, in1=cb, op=ALU.mult)
            P.op('dve', 'tensor_tensor', R=[srcB, csB], W=[t2], out=t2[:, 0:H, :], in0=x2, in1=sb_, op=ALU.mult)
            P.op('dve', 'tensor_tensor', R=[t1, t2], W=[dstB], out=dst[:, :, 0:64], in0=t1[:, 0:H, :], in1=t2[:, 0:H, :], op=ALU.subtract)
            P.op('pool', 'tensor_tensor', R=[srcB, csB], W=[t3], out=t3[:, 0:H, :], in0=x1, in1=sb_, op=ALU.mult)
            P.op('pool', 'tensor_tensor', R=[srcB, csB], W=[t4], out=t4[:, 0:H, :], in0=x2, in1=cb, op=ALU.mult)
            P.op('pool', 'tensor_tensor', R=[t3, t4], W=[dstB], out=dst[:, :, 64:128], in0=t3[:, 0:H, :], in1=t4[:, 0:H, :], op=ALU.add)


        with ExitStack() as sa:
            Wkv = SB("Wkv", [128, 16, 2048], BF16, sa)
            rt = [SB("rtA%d" % i, [128, 8, 64], F32, sa) for i in range(4)]
            csA = SB("csA", [128, 2, 8 * NBLK, 64], F32, sa)
            xa = [SB("xa%d" % i, [128, 16, 128], BF16, sa) for i in range(2)]
            ksb = SB("ksb", [128, 8, 128], F32, sa)
            kr = SB("kr", [128, 8, 128], F32, sa)
            kfA = SB("kfA", [128, 8, 128], BF16, sa)
            kbA = SB("kbA", [128, 8, 128], BF16, sa)
            vA = SB("vA", [128, 1024], BF16, sa)
            SinAcc = [[SB("sin%d_%d" % (s, d), [128, 1024], F32, sa) for d in range(2)] for s in range(NSEG)]
            pk = PS("pk", [128, 1024], F32, sa)
            pvv = PS("pvv", [128, 1024], F32, sa)
            pAf = PS("pAf", [128, 1024], F32, sa)
            pAb = PS("pAb", [128, 1024], F32, sa)
            w_kv = w_in[:, 1024:3072].rearrange("(c p) n -> p c n", p=128)
            for c4 in range(4):
                P.dma('pool', Wkv, Wkv[:, 4 * c4:4 * c4 + 4, :], w_kv[:, 4 * c4:4 * c4 + 4, :], W=[Wkv])
            P.dma('sp', csA, csA[:], csall, W=[csA])
            for s in range(NSEG):
                for d in range(2):
                    P.op('pool', 'memset', W=[SinAcc[s][d]], ap=SinAcc[s][d][:], constant=0.0)
            it = 0
            for s in range(NSEG):
                for r in range(NBLK):
                    if (s, r) not in A_BLOCKS:
                        continue
                    for m in range(NCH):
                        g = 8 * r + m
                        xb_ = xa[it % 2]
                        it += 1
                        P.dma('pool', xb_, xb_[:], xTall[s, :, g * 128:(g + 1) * 128].rearrange("(c p) t -> p c t", p=128), W=[xb_])
                        conv_step(1)
                        for half in range(2):
                            for c in range(16):
                                P.op('pe', 'matmul', R=[xb_, Wkv], W=[pk], out=pk[:, half * 512:(half + 1) * 512], lhsT=xb_[:, c, :],
                                     rhs=Wkv[:, c, half * 512:(half + 1) * 512], start=(c == 0), stop=(c == 15))
                        for half in range(2):
                            for c in range(16):
                                P.op('pe', 'matmul', R=[xb_, Wkv], W=[pvv], out=pvv[:, half * 512:(half + 1) * 512], lhsT=xb_[:, c, :],
                                     rhs=Wkv[:, c, 1024 + half * 512:1024 + (half + 1) * 512], start=(c == 0), stop=(c == 15))
                        P.op('act', 'activation', R=[pk], W=[ksb], out=ksb[:].rearrange("p h d -> p (h d)"), in_=pk[:], func=AF.Copy)
                        P.op('act', 'activation', R=[pvv], W=[vA], out=vA[:], in_=pvv[:], func=AF.Copy)
                        rotary(ksb[:], kr[:], csA[:, 0, g, :], csA[:, 1, g, :], 8, rt[0], rt[1], rt[2], rt[3], ksb, kr, csA)
                        P.op('dve', 'tensor_tensor', R=[kr, ZA], W=[kfA], out=kfA[:], in0=kr[:],
                             in1=ZA[:, 0, m, :].unsqueeze(2).to_broadcast([128, 8, 128]), op=ALU.mult)
                        P.op('pool', 'tensor_tensor', R=[kr, ZA], W=[kbA], out=kbA[:], in0=kr[:],
                             in1=ZA[:, 1, m, :].unsqueeze(2).to_broadcast([128, 8, 128]), op=ALU.mult)
                        for h in range(8):
                            P.op('pe', 'matmul', R=[kfA, vA], W=[pAf], out=pAf[:, h * 128:(h + 1) * 128], lhsT=kfA[:, h, :],
                                 rhs=vA[:, h * 128:(h + 1) * 128], start=(m == 0 and h % 4 == 0), stop=(m == 7), skip_group_check=True)
                            P.op('pe', 'matmul', R=[kbA, vA], W=[pAb], out=pAb[:, h * 128:(h + 1) * 128], lhsT=kbA[:, h, :],
                                 rhs=vA[:, h * 128:(h + 1) * 128], start=(m == 0 and h % 4 == 0), stop=(m == 7), skip_group_check=True)
                    for d, pA in ((0, pAf), (1, pAb)):
                        acc = SinAcc[s][d]
                        for h in range(8):
                            P.op('dve', 'scalar_tensor_tensor', R=[pA, Wc, acc], W=[acc], out=acc[:, h * 128:(h + 1) * 128],
                                 in0=pA[:, h * 128:(h + 1) * 128], scalar=Wc[:, d, r, h:h + 1], in1=acc[:, h * 128:(h + 1) * 128],
                                 op0=ALU.mult, op1=ALU.add)
            dump('kr', kr, kr[:], [128, 8, 128])
            dump('sin00', SinAcc[0][0], SinAcc[0][0][:], [128, 1024])
            dump('sin01', SinAcc[0][1], SinAcc[0][1][:], [128, 1024])
            for s in range(NSEG):
                for d in range(2):
                    if not dbg.get('sin_in'):
                        P.dma('sp', SinAcc[s][d], sin_d[s, d], SinAcc[s][d][:], R=[SinAcc[s][d]])
            conv_step(10 ** 6)
            P.barrier()

        catT = SB("catT", [128, 16, 1024], BF16)
        WB = [SB("WB%d" % i, [128, 16, 512], BF16) for i in range(2)]
        wbi = [0]

        def load_wb(src_cols):
            b = WB[wbi[0] % 2]
            wbi[0] += 1
            for apx, off, n in src_cols:
                P.dma('sp', b, b[:, :, off:off + n], apx.rearrange("(c p) n -> p c n", p=128), R=[W16], W=[b])
            return b

        pM = PS("pM", [128, 2048], F32)
        pX0 = PS("pX0", [128, 512], F32)
        pX1 = PS("pX1", [128, 512], F32)
        pX2 = PS("pX2", [128, 512], F32)
        pT = PS("pT", [128, 1024], BF16)

        for s in SEGS:
            with ExitStack() as s1:
                xT = SB("xT", [128, 16, 1024], BF16, s1)
                xH = SB("xH", [128, 16, 64], BF16, s1)
                rt = [SB("rtB%d" % i, [128, 8, 64], F32, s1) for i in range(4)]
                for c4 in range(4):
                    P.dma('pool', xT, xT[:, 4 * c4:4 * c4 + 4, :],
                          xTs[s].rearrange("(c p) t -> p c t", p=128)[:, 4 * c4:4 * c4 + 4, :], W=[xT])
                P.dma('pool', xH, xH[:], xTh[s].rearrange("(c p) t -> p c t", p=128), W=[xH])
                QT = SB("QT", [128, 8, 128], BF16, s1)
                QTF = SB("QTF", [128, 8, 128], BF16, s1)
                QTB = SB("QTB", [128, 8, 128], BF16, s1)
                KT = SB("KT", [128, 8, 128], BF16, s1)
                KF = SB("KF", [128, 8, 128], BF16, s1)
                KB = SB("KB", [128, 8, 128], BF16, s1)
                VB = SB("VB", [128, 8, 128], BF16, s1)
                GS = SB("GS", [128, 8, 128], F32, s1)
                qkA = SB("qkA", [128, 8, 2, 128], F32, s1)
                qkR = SB("qkR", [128, 8, 2, 128], F32, s1)
                qkB = SB("qkB", [128, 8, 2, 128], BF16, s1)
                Oall = SB("Oall", [128, 8, 128], F32, s1)
                RO = SB("RO", [128, 8, 128], BF16, s1)
                SbN = [SB("SbN%d" % n, [128, 128], BF16, s1) for n in range(NCH)]
                SfN = SB("SfN", [128, 128], BF16, s1)
                Sf = SB("Sf", [128, 128], F32, s1)
                Sb = SB("Sb", [128, 128], F32, s1)
                scm = SB("scm", [128, 128], BF16, s1)
                st1 = SB("st1", [128, 4, 8], F32, s1)
                hcs = SB("hcs", [128, 1026], F32, s1)
                us = SB("us", [128, 1026], F32, s1)
                bgs = SB("bgs", [128, 1024], F32, s1)
                cv = SB("cv", [128, 1024], F32, s1)

                for h in HEADS:
                    wb = load_wb([(w_in16[:, j * 1024 + h * 128: j * 1024 + (h + 1) * 128], j * 128, 128) for j in range(4)])
                    hs = slice(h * 128, (h + 1) * 128)
                    P.dma('sp', Sf, Sf[:], sin_d[s, 0][:, hs], W=[Sf])
                    P.dma('sp', Sb, Sb[:], sin_d[s, 1][:, hs], W=[Sb])
                    for n in range(NCH):
                        pp = pX0 if n % 2 == 0 else pX2
                        for c in range(16):
                            P.op('pe', 'matmul', R=[xT, wb], W=[pp], out=pp[:], lhsT=xT[:, c, 128 * n:128 * (n + 1)],
                                 rhs=wb[:, c, :], start=(c == 0), stop=(c == 15))
                        P.op('act', 'activation', R=[pp], W=[qkA], out=qkA[:, n].rearrange("p a d -> p (a d)"), in_=pp[:, 0:256], func=AF.Copy)
                        P.op('act', 'activation', R=[pp], W=[VB], out=VB[:, n, :], in_=pp[:, 256:384], func=AF.Copy)
                        P.op('act', 'activation', R=[pp], W=[GS], out=GS[:, n, :], in_=pp[:, 384:512], func=AF.Silu)
                    if HST < 1:
                        continue
                    for r4 in range(2):
                        cs_ = slice(4 * r4, 4 * r4 + 4)
                        src = qkA[:, cs_]
                        dst = qkR[:, cs_]
                        cb = cso[:, 0, cs_, :].unsqueeze(2).to_broadcast([128, 4, 2, 64])
                        sb_ = cso[:, 1, cs_, :].unsqueeze(2).to_broadcast([128, 4, 2, 64])
                        x1 = src[:, :, :, 0:64]
                        x2 = src[:, :, :, 64:128]
                        tv = [t[:].rearrange("p (a b) d -> p a b d", b=2) for t in rt]
                        P.op('dve', 'tensor_tensor', R=[qkA, cso], W=[rt[0]], out=tv[0], in0=x1, in1=cb, op=ALU.mult)
                        P.op('dve', 'tensor_tensor', R=[qkA, cso], W=[rt[1]], out=tv[1], in0=x2, in1=sb_, op=ALU.mult)
                        P.op('dve', 'tensor_tensor', R=[rt[0], rt[1]], W=[qkR], out=dst[:, :, :, 0:64], in0=tv[0], in1=tv[1], op=ALU.subtract)
                        P.op('pool', 'tensor_tensor', R=[qkA, cso], W=[rt[2]], out=tv[2], in0=x1, in1=sb_, op=ALU.mult)
                        P.op('pool', 'tensor_tensor', R=[qkA, cso], W=[rt[3]], out=tv[3], in0=x2, in1=cb, op=ALU.mult)
                        P.op('pool', 'tensor_tensor', R=[rt[2], rt[3]], W=[qkR], out=dst[:, :, :, 64:128], in0=tv[2], in1=tv[3], op=ALU.add)
                    if HST < 2:
                        continue
                    P.op('dve', 'tensor_copy', R=[qkR], W=[qkB], out=qkB[:], in_=qkR[:])
                    P.op('dve', 'tensor_scalar', R=[qkR, Z1], W=[KF], out=KF[:], in0=qkR[:, :, 1, :], scalar1=Z1[:, h:h + 1], scalar2=None, op0=ALU.mult)
                    P.op('pool', 'tensor_scalar', R=[qkR, Z1], W=[KB], out=KB[:], in0=qkR[:, :, 1, :], scalar1=Z1[:, 8 + h:9 + h], scalar2=None, op0=ALU.mult)
                    if HST < 3:
                        continue
                    for r4 in range(2):
                        cs_ = slice(4 * r4, 4 * r4 + 4)
                        for n in range(4 * r4, 4 * r4 + 4):
                            for a_ in range(2):
                                o0 = (n % 4) * 256 + a_ * 128
                                P.op('pe', 'transpose', R=[qkB, idb], W=[pT], out=pT[:, o0:o0 + 128], in_=qkB[:, n, a_, :], identity=idb[:])
                        pT4 = pT[:].rearrange("p (n a d) -> p n a d", n=4, a=2)
                        P.op('act', 'activation', R=[pT], W=[QT], out=QT[:, cs_, :], in_=pT4[:, :, 0, :], func=AF.Copy)
                        P.op('dve', 'tensor_tensor', R=[pT, XiF], W=[QTF], out=QTF[:, cs_, :], in0=pT4[:, :, 0, :],
                             in1=XiF[:, h, :].unsqueeze(1).to_broadcast([128, 4, 128]), op=ALU.mult)
                        P.op('dve', 'tensor_tensor', R=[pT, XiB], W=[QTB], out=QTB[:, cs_, :], in0=pT4[:, :, 0, :],
                             in1=XiB[:, h, :].unsqueeze(1).to_broadcast([128, 4, 128]), op=ALU.mult)
                        P.op('act', 'activation', R=[pT], W=[KT], out=KT[:, cs_, :], in_=pT4[:, :, 1, :], func=AF.Copy)
                    if HST < 4:
                        continue
                    for n in range(NCH - 1, -1, -1):
                        P.op('act', 'activation', R=[Sb], W=[SbN[n]], out=SbN[n][:], in_=Sb[:], func=AF.Copy)
                        if n > 0:
                            P.op('pe', 'matmul', R=[KB, VB], W=[pX2], out=pX2[:, 0:128], lhsT=KB[:, n, :], rhs=VB[:, n, :], start=True, stop=True)
                            P.op('dve', 'scalar_tensor_tensor', R=[Sb, G128, pX2], W=[Sb], out=Sb[:], in0=Sb[:], scalar=G128[:, 8 + h:9 + h],
                                 in1=pX2[:, 0:128], op0=ALU.mult, op1=ALU.add)
                    for n in range(NCH):
                        P.op('act', 'activation', R=[Sf], W=[SfN], out=SfN[:], in_=Sf[:], func=AF.Copy)
                        P.op('pe', 'matmul', R=[KT, QT], W=[pX1], out=pX1[:, 0:128], lhsT=KT[:, n, :], rhs=QT[:, n, :], start=True, stop=True)
                        P.op('dve', 'tensor_tensor', R=[pX1, DmT], W=[scm], out=scm[:], in0=pX1[:, 0:128], in1=DmT[:, h, :], op=ALU.mult)
                        P.op('pe', 'matmul', R=[scm, VB], W=[pX1], out=pX1[:, 128:256], lhsT=scm[:], rhs=VB[:, n, :], start=True, stop=False)
                        P.op('pe', 'matmul', R=[QTF, SfN], W=[pX1], out=pX1[:, 128:256], lhsT=QTF[:, n, :], rhs=SfN[:], start=False, stop=False)
                        P.op('pe', 'matmul', R=[QTB, SbN[n]], W=[pX1], out=pX1[:, 128:256], lhsT=QTB[:, n, :], rhs=SbN[n][:], start=False, stop=True)
                        if n < NCH - 1:
                            P.op('pe', 'matmul', R=[KF, VB], W=[pX2], out=pX2[:, 128:256], lhsT=KF[:, n, :], rhs=VB[:, n, :], start=True, stop=True)
                            P.op('dve', 'scalar_tensor_tensor', R=[Sf, G128, pX2], W=[Sf], out=Sf[:], in0=Sf[:], scalar=G128[:, h:h + 1],
                                 in1=pX2[:, 128:256], op0=ALU.mult, op1=ALU.add)
                        P.op('act', 'activation', R=[pX1], W=[Oall], out=Oall[:, n, :], in_=pX1[:, 128:256], func=AF.Copy)
                    if HST < 5:
                        continue
                    sq = qkA[:].rearrange("p n a d -> p (n a d)")[:, 0:1024].rearrange("p (n d) -> p n d", n=8)
                    P.op('dve', 'tensor_reduce', R=[Oall], W=[st1], out=st1[:, 0, :], in_=Oall[:], axis=AX.X, op=ALU.add)
                    P.op('dve', 'tensor_scalar', R=[st1], W=[st1], out=st1[:, 1, :], in0=st1[:, 0, :], scalar1=-1.0 / 128, scalar2=None, op0=ALU.mult)
                    P.op('dve', 'tensor_tensor', R=[Oall, st1], W=[Oall], out=Oall[:], in0=Oall[:], in1=st1[:, 1, :].unsqueeze(2).to_broadcast([128, 8, 128]), op=ALU.add)
                    if HST < 5.2:
                        continue
                    P.op('pool', 'tensor_tensor', R=[Oall], W=[qkA], out=sq, in0=Oall[:], in1=Oall[:], op=ALU.mult)
                    P.op('dve', 'tensor_reduce', R=[qkA], W=[st1], out=st1[:, 2, :], in_=sq, axis=AX.X, op=ALU.add)
                    P.op('dve', 'tensor_scalar', R=[st1], W=[st1], out=st1[:, 3, :], in0=st1[:, 2, :], scalar1=1.0 / 128, scalar2=GN_EPS, op0=ALU.mult, op1=ALU.add)
                    if HST < 5.3:
                        continue
                    P.op('act', 'activation', R=[st1], W=[st1], out=st1[:, 3, :], in_=st1[:, 3, :], func=AF.Sqrt)
                    P.op('dve', 'reciprocal', R=[st1], W=[st1], out=st1[:, 3, :], in_=st1[:, 3, :])
                    if HST < 5.4:
                        continue
                    P.op('pool', 'tensor_tensor', R=[GS, gn], W=[GS], out=GS[:], in0=GS[:], in1=gn[:, hs].unsqueeze(1).to_broadcast([128, 8, 128]), op=ALU.mult)
                    P.op('dve', 'tensor_tensor', R=[Oall, st1], W=[Oall], out=Oall[:], in0=Oall[:], in1=st1[:, 3, :].unsqueeze(2).to_broadcast([128, 8, 128]), op=ALU.mult)
                    P.op('dve', 'tensor_tensor', R=[Oall, GS], W=[RO], out=RO[:], in0=Oall[:], in1=GS[:], op=ALU.mult)
                    if HST < 5.5:
                        continue
                    for n in range(NCH):
                        P.op('pe', 'transpose', R=[RO, idb], W=[pT], out=pT[:, n * 128:(n + 1) * 128], in_=RO[:, n, :], identity=idb[:])
                    P.op('act', 'activation', R=[pT], W=[catT], out=catT[:, h, :], in_=pT[:], func=AF.Copy)
                pcs = [pX0, pX1]
                for gi in GROUPS:
                    wb = load_wb([(w_in16[:, 4096 + j * 1024 + gi * 128: 4096 + j * 1024 + (gi + 1) * 128], j * 128, 128) for j in range(3)])
                    for j in (0, 2, 1):
                        for si in range(2):
                            pc = pcs[si]
                            c0 = 512 * si
                            for c in range(16):
                                P.op('pe', 'matmul', R=[xT, wb], W=[pc], out=pc[:], lhsT=wb[:, c, j * 128:(j + 1) * 128],
                                     rhs=xT[:, c, c0:c0 + 512], start=(c == 0), stop=(c == 15))
                            if j == 0:
                                P.op('act', 'activation', R=[pc], W=[hcs], out=hcs[:, 1 + c0:1 + c0 + 512], in_=pc[:], func=AF.Copy)
                            elif j == 2:
                                P.op('dve', 'tensor_tensor', R=[pc, hcs], W=[us], out=us[:, 1 + c0:1 + c0 + 512], in0=pc[:], in1=hcs[:, 1 + c0:1 + c0 + 512], op=ALU.mult)
                            else:
                                P.op('act', 'activation', R=[pc], W=[bgs], out=bgs[:, c0:c0 + 512], in_=pc[:], func=AF.Copy)
                        if j != 1:
                            for c in range(16):
                                P.op('pe', 'matmul', R=[xH, wb], W=[pX2], out=pX2[:, 0:2], lhsT=wb[:, c, j * 128:(j + 1) * 128],
                                     rhs=xH[:, c, 0:2], start=(c == 0), stop=(c == 15))
                            if j == 0:
                                P.op('act', 'activation', R=[pX2], W=[hcs], out=hcs[:, 0:1], in_=pX2[:, 0:1], func=AF.Copy)
                                P.op('act', 'activation', R=[pX2], W=[hcs], out=hcs[:, 1025:1026], in_=pX2[:, 1:2], func=AF.Copy)
                            else:
                                P.op('dve', 'tensor_tensor', R=[pX2, hcs], W=[us], out=us[:, 0:1], in0=pX2[:, 0:1], in1=hcs[:, 0:1], op=ALU.mult)
                                P.op('dve', 'tensor_tensor', R=[pX2, hcs], W=[us], out=us[:, 1025:1026], in0=pX2[:, 1:2], in1=hcs[:, 1025:1026], op=ALU.mult)
                    P.op('dve', 'tensor_scalar', R=[us, cw], W=[cv], out=cv[:], in0=us[:, 0:1024], scalar1=cw[:, gi, 0:1], scalar2=None, op0=ALU.mult)
                    P.op('dve', 'scalar_tensor_tensor', R=[us, cw, cv], W=[cv], out=cv[:], in0=us[:, 1:1025], scalar=cw[:, gi, 1:2], in1=cv[:], op0=ALU.mult, op1=ALU.add)
                    P.op('dve', 'scalar_tensor_tensor', R=[us, cw, cv], W=[cv], out=cv[:], in0=us[:, 2:1026], scalar=cw[:, gi, 2:3], in1=cv[:], op0=ALU.mult, op1=ALU.add)
                    P.op('dve', 'tensor_tensor', R=[cv, bgs], W=[catT], out=catT[:, 8 + gi, :], in0=cv[:], in1=bgs[:], op=ALU.mult)
                dump('catT', catT, catT[:], [128, 16, 1024], BF16)
                P.barrier()

            with ExitStack() as s2:
                xt = SB("xt", [128, 2048], F32, s2)
                z = SB("z", [128, 2048], F32, s2)
                zq = SB("zq", [128, 2048], F32, s2)
                hh = SB("hh", [128, 2048], F32, s2)
                hb = SB("hb", [128, 2048], BF16, s2)
                hT = SB("hT", [128, 16, 128], BF16, s2)
                qp = SB("qp", [128, 2048], BF16, s2)
                qpT = hT
                vG = SB("vG", [128, 2048], F32, s2)
                vB = vG
                st2 = SB("st2", [128, 4], F32, s2)
                v16 = SB("v16", [128, 16, 16], F32, s2)
                ix16 = SB("ix16", [128, 16, 16], U32, s2)
                ixf = SB("ixf", [128, 16, 16], F32, s2)
                tmp = SB("tmp", [128, 256], F32, s2)
                best = SB("best", [128, 8, 16], F32, s2)
                pos = SB("pos", [128, 8, 16], U32, s2)
                pab = SB("pab", [128, 2, 8, 16], U32, s2)
                pabf = SB("pabf", [128, 2, 8, 16], F32, s2)
                isel = SB("isel", [128, 2, 8, 16], F32, s2)
                eidf = SB("eidf", [128, 128], F32, s2)
                gate = SB("gate", [128, 8, 16], F32, s2)
                gst = SB("gst", [128, 16], F32, s2)
                eidI = SB("eidI", [128, 128], I32, s2)
                actT = SB("actT", [128, 128], F32, s2)
                coefT = SB("coefT", [128, 128], F32, s2)
                idf = SB("idf", [128, 128], F32, s2)
                UVs = [SB("UV%d" % i, [128, 4096], BF16, s2) for i in range(NUV)]
                print("sbuf remaining (scope2)", nc.sbuf_bytes_remaining)
                actC = [Buf(actT.t) for _ in range(128)]
                coefC = [Buf(coefT.t) for _ in range(128)]
                dg = [SB("dg%d" % i, [128, 128], BF16, s2) for i in range(3)]
                P.op('dve', 'tensor_copy', R=[ct], W=[idf], out=idf[:], in_=ct[:, C_ID:C_ID + 128])
                cand = z
                oh = zq

                def layer_norm(src, dst, gi, extra_dst=None):
                    P.dma('sp', vG, vG[:], lnv[gi], W=[vG])
                    P.op('dve', 'reduce_sum', R=[src], W=[st2], out=st2[:, 0:1], in_=src[:], axis=AX.X)
                    P.op('dve', 'tensor_scalar', R=[st2], W=[st2], out=st2[:, 1:2], in0=st2[:, 0:1], scalar1=-1.0 / 2048, scalar2=None, op0=ALU.mult)
                    P.op('dve', 'tensor_scalar', R=[src, st2], W=[src], out=src[:], in0=src[:], scalar1=st2[:, 1:2], scalar2=None, op0=ALU.add)
                    P.op('pool', 'tensor_tensor', R=[src], W=[zq], out=zq[:], in0=src[:], in1=src[:], op=ALU.mult)
                    P.op('dve', 'reduce_sum', R=[zq], W=[st2], out=st2[:, 2:3], in_=zq[:], axis=AX.X)
                    P.op('dve', 'tensor_scalar', R=[st2], W=[st2], out=st2[:, 3:4], in0=st2[:, 2:3], scalar1=1.0 / 2048, scalar2=LN_EPS, op0=ALU.mult, op1=ALU.add)
                    P.op('act', 'activation', R=[st2], W=[st2], out=st2[:, 3:4], in_=st2[:, 3:4], func=AF.Sqrt)
                    P.op('dve', 'reciprocal', R=[st2], W=[st2], out=st2[:, 3:4], in_=st2[:, 3:4])
                    P.op('dve', 'scalar_tensor_tensor', R=[src, st2, vG], W=[dst], out=dst[:], in0=src[:], scalar=st2[:, 3:4], in1=vG[:], op0=ALU.mult, op1=ALU.mult)
                    P.dma('sp', vB, vB[:], lnv[gi + 1], W=[vB])
                    P.op('pool', 'tensor_tensor', R=[dst, vB], W=[dst], out=dst[:], in0=dst[:], in1=vB[:], op=ALU.add)

                if cat_in is not None:
                    P.dma('pool', catT, catT[:], cat_in, W=[catT])
                for n in TILES:
                    ts_ = slice(128 * n, 128 * (n + 1))
                    P.dma('sp', xt, xt[:], xtok[s, ts_, :], W=[xt])
                    for cb in range(4):
                        wb = load_wb([(w_out16[:, cb * 512:(cb + 1) * 512], 0, 512)])
                        for c in range(16):
                            P.op('pe', 'matmul', R=[catT, wb], W=[pM], out=pM[:, cb * 512:(cb + 1) * 512], lhsT=catT[:, c, ts_],
                                 rhs=wb[:, c, :], start=(c == 0), stop=(c == 15))
                    P.op('dve', 'scalar_tensor_tensor', R=[xt, pM], W=[z], out=z[:], in0=xt[:], scalar=ALPHA, in1=pM[:], op0=ALU.mult, op1=ALU.add)
                    layer_norm(z, hh, 0)
                    dump('h1', hh, hh[:], [128, 2048])
                    P.op('act', 'activation', R=[hh], W=[hb], out=hb[:], in_=hh[:], func=AF.Copy)
                    for c in range(16):
                        P.op('pe', 'transpose', R=[hb, idb], W=[pT], out=pT[:, (c % 8) * 128:(c % 8 + 1) * 128], in_=hb[:, c * 128:(c + 1) * 128], identity=idb[:])
                        if c % 8 == 7:
                            c0 = c - 7
                            P.op('act', 'activation', R=[pT], W=[hT], out=hT[:, c0:c0 + 8, :].rearrange("p c t -> p (c t)"), in_=pT[:], func=AF.Copy)
                    for cb in range(4):
                        wb = load_wb([(w_q16[:, cb * 512:(cb + 1) * 512], 0, 512)])
                        for c in range(16):
                            P.op('pe', 'matmul', R=[hT, wb], W=[pX0], out=pX0[:], lhsT=hT[:, c, :], rhs=wb[:, c, :], start=(c == 0), stop=(c == 15))
                        P.op('act', 'activation', R=[pX0], W=[qp], out=qp[:, cb * 512:(cb + 1) * 512], in_=pX0[:], func=AF.Copy)
                    for c in range(16):
                        P.op('pe', 'transpose', R=[qp, idb], W=[pT], out=pT[:, (c % 8) * 128:(c % 8 + 1) * 128], in_=qp[:, c * 128:(c + 1) * 128], identity=idb[:])
                        if c % 8 == 7:
                            c0 = c - 7
                            P.op('act', 'activation', R=[pT], W=[qpT], out=qpT[:, c0:c0 + 8, :].rearrange("p c t -> p (c t)"), in_=pT[:], func=AF.Copy)
                    for mm in range(16):
                        P.op('pe', 'matmul', R=[qpT, kT16], W=[pM], out=pM[:, mm * 128:(mm + 1) * 128], lhsT=qpT[:, mm, :], rhs=kT16[:, mm, :], start=True, stop=True)
                    sc = xt
                    P.op('act', 'activation', R=[pM], W=[sc], out=sc[:], in_=pM[:], func=AF.Copy)
                    dump('sc', sc, sc[:], [128, 2048])
                    for mm in range(16):
                        srow = sc[:, mm * 128:(mm + 1) * 128]
                        P.op('dve', 'max', R=[sc], W=[v16], out=v16[:, mm, 0:8], in_=srow)
                        P.op('dve', 'max_index', R=[sc, v16], W=[ix16], out=ix16[:, mm, 0:8], in_max=v16[:, mm, 0:8], in_values=srow)
                        P.op('dve', 'match_replace', R=[sc, v16], W=[tmp], out=tmp[:, 0:128], in_to_replace=v16[:, mm, 0:8], in_values=srow, imm_value=-1e30)
                        P.op('dve', 'max', R=[tmp], W=[v16], out=v16[:, mm, 8:16], in_=tmp[:, 0:128])
                        P.op('dve', 'max_index', R=[tmp, v16], W=[ix16], out=ix16[:, mm, 8:16], in_max=v16[:, mm, 8:16], in_values=tmp[:, 0:128])
                    P.op('dve', 'tensor_copy', R=[ix16], W=[ixf], out=ixf[:], in_=ix16[:])
                    v4 = v16[:].rearrange("p (h two) k -> p h two k", two=2)
                    i4 = ixf[:].rearrange("p (h two) k -> p h two k", two=2)
                    c4 = cand[:].rearrange("p (h a b) -> p h a b", h=8, a=16)
                    P.op('dve', 'tensor_tensor', R=[v16], W=[cand], out=c4, in0=v4[:, :, 0, :].unsqueeze(3).to_broadcast([128, 8, 16, 16]),
                         in1=v4[:, :, 1, :].unsqueeze(2).to_broadcast([128, 8, 16, 16]), op=ALU.add)
                    for h in range(8):
                        crow = cand[:, h * 256:(h + 1) * 256]
                        P.op('dve', 'max', R=[cand], W=[best], out=best[:, h, 0:8], in_=crow)
                        P.op('dve', 'max_index', R=[cand, best], W=[pos], out=pos[:, h, 0:8], in_max=best[:, h, 0:8], in_values=crow)
                        P.op('dve', 'match_replace', R=[cand, best], W=[tmp], out=tmp[:], in_to_replace=best[:, h, 0:8], in_values=crow, imm_value=-1e30)
                        P.op('dve', 'max', R=[tmp], W=[best], out=best[:, h, 8:16], in_=tmp[:])
                        P.op('dve', 'max_index', R=[tmp, best], W=[pos], out=pos[:, h, 8:16], in_max=best[:, h, 8:16], in_values=tmp[:])
                    P.op('dve', 'tensor_single_scalar', R=[pos], W=[pab], out=pab[:, 0], in_=pos[:], scalar=4, op=ALU.logical_shift_right)
                    P.op('dve', 'tensor_single_scalar', R=[pos], W=[pab], out=pab[:, 1], in_=pos[:], scalar=15, op=ALU.bitwise_and)
                    P.op('dve', 'tensor_copy', R=[pab], W=[pabf], out=pabf[:], in_=pab[:])
                    o4 = oh[:].rearrange("p (h k a) -> p h k a", h=8, k=16)
                    iob = ct[:, C_IOTA:C_IOTA + 16].unsqueeze(1).unsqueeze(1).to_broadcast([128, 8, 16, 16])
                    for w_ in range(2):
                        P.op('dve', 'tensor_tensor', R=[pabf, ct], W=[oh], out=o4, in0=pabf[:, w_].unsqueeze(3).to_broadcast([128, 8, 16, 16]), in1=iob, op=ALU.is_equal)
                        P.op('dve', 'tensor_tensor', R=[oh, ixf], W=[oh], out=o4, in0=o4, in1=i4[:, :, w_, :].unsqueeze(2).to_broadcast([128, 8, 16, 16]), op=ALU.mult)
                        P.op('dve', 'tensor_reduce', R=[oh], W=[isel], out=isel[:, w_], in_=o4, axis=AX.X, op=ALU.add)
                    P.op('dve', 'scalar_tensor_tensor', R=[isel], W=[eidf], out=eidf[:].rearrange("p (h k) -> p h k", h=8), in0=isel[:, 0], scalar=128.0, in1=isel[:, 1], op0=ALU.mult, op1=ALU.add)
                    P.op('dve', 'tensor_tensor', R=[best], W=[gate], out=gate[:], in0=best[:], in1=best[:, :, 0:1].to_broadcast([128, 8, 16]), op=ALU.subtract)
                    P.op('act', 'activation', R=[gate], W=[gate], out=gate[:], in_=gate[:], func=AF.Exp)
                    P.op('dve', 'tensor_reduce', R=[gate], W=[gst], out=gst[:, 0:8], in_=gate[:], axis=AX.X, op=ALU.add)
                    P.op('dve', 'reciprocal', R=[gst], W=[gst], out=gst[:, 8:16], in_=gst[:, 0:8])
                    P.op('dve', 'tensor_tensor', R=[gate, gst], W=[gate], out=gate[:], in0=gate[:], in1=gst[:, 8:16].unsqueeze(2).to_broadcast([128, 8, 16]), op=ALU.mult)
                    dump('eidf', eidf, eidf[:], [128, 128])
                    dump('gate', gate, gate[:], [128, 8, 16])
                    P.op('dve', 'tensor_copy', R=[eidf], W=[eidI], out=eidI[:], in_=eidf[:])
                    P.op('pool', 'memset', W=actC, ap=actT[:], constant=0.0)
                    gate2 = gate[:].rearrange("p h k -> p (h k)")

                    def gUV(k):
                        ub = UVs[k % NUV]
                        P.dma('pool', ub, ub[:], uv16, R=[eidI, UV16], W=[ub], name='indirect_dma_start', out_offset=None,
                              in_offset=bass.IndirectOffsetOnAxis(ap=eidI[:, k:k + 1], axis=0))

                    def vstep(k):
                        vb_ = UVs[k % NUV]
                        dgb = dg[k % len(dg)]
                        P.op('act', 'activation', R=[coefC[k], gate], W=[coefC[k]], out=coefT[:, k:k + 1], in_=coefT[:, k:k + 1], func=AF.Copy, scale=gate2[:, k:k + 1])
                        P.op('act', 'activation', R=[idf, coefC[k]], W=[dgb], out=dgb[:], in_=idf[:], func=AF.Copy, scale=coefT[:, k:k + 1])
                        for q4 in range(4):
                            P.op('pe', 'matmul', R=[dgb, vb_], W=[pM], out=pM[:, q4 * 512:(q4 + 1) * 512], lhsT=dgb[:], rhs=vb_[:, 2048 + q4 * 512:2048 + (q4 + 1) * 512],
                                 start=(k == 0), stop=(k == NTOK - 1))

                    PF = NUV - 2
                    for k in range(min(PF, NTOK)):
                        gUV(k)
                    for k in range(NTOK):
                        if k + PF < NTOK:
                            gUV(k + PF)
                        ub = UVs[k % NUV]
                        P.op('dve', 'scalar_tensor_tensor', R=[ub, hh], W=[zq, actC[k]], out=zq[:], in0=ub[:, 0:2048], scalar=1.0, in1=hh[:], op0=ALU.mult, op1=ALU.mult,
                             accum_out=actT[:, k:k + 1])
                        P.op('act', 'activation', R=[actC[k]], W=[coefC[k]], out=coefT[:, k:k + 1], in_=actT[:, k:k + 1], func=AF.Gelu)
                        if k >= 1:
                            vstep(k - 1)
                    vstep(NTOK - 1)
                    if 'actT' in dbg.get('dump', []):
                        P.op('dve', 'tensor_copy', R=actC, W=[actT], out=actT[:], in_=actT[:])
                    dump('actT', actT, actT[:], [128, 128])
                    P.op('dve', 'scalar_tensor_tensor', R=[hh, pM], W=[z], out=z[:], in0=hh[:], scalar=ALPHA, in1=pM[:], op0=ALU.mult, op1=ALU.add)
                    dump('z2', z, z[:], [128, 2048])
                    layer_norm(z, hh, 2)
                    P.dma('pool', hh, y[s, ts_, :], hh[:], R=[hh])
                P.barrier()
        P.barrier()

        with nc.Block() as block:
            @block.tensor
            def _(E):
                P.replay('pe', E)

            @block.vector
            def _(E):
                P.replay('dve', E)

            @block.scalar
            def _(E):
                P.replay('act', E)

            @block.gpsimd
            def _(E):
                P.replay('pool', E)

            @block.sync
            def _(E):
                P.replay('sp', E)
    return nc


def _host_tables(c):
    j = np.arange(128, dtype=np.float64)[:, None]
    i = np.arange(128, dtype=np.float64)[None, :]
    t = np.zeros((128, NT), np.float32)
    t[:, C_EPF:C_EPF + 128] = np.maximum(i - j, 0)
    t[:, C_MPF:C_MPF + 128] = (i >= j)
    t[:, C_EPB:C_EPB + 128] = np.maximum(j - i, 0)
    t[:, C_MPB:C_MPB + 128] = (j >= i)
    t[:, C_EXF:C_EXF + 128] = i + 1
    t[:, C_EXB:C_EXB + 128] = 128 - i
    t[:, C_EZF1] = 127 - j[:, 0]
    t[:, C_EZB1] = j[:, 0]
    m = np.arange(8)[None, :]
    t[:, C_EZFA:C_EZFA + 8] = 1023 - (128 * m + j)
    t[:, C_EZBA:C_EZBA + 8] = 128 * m + j
    rl = np.arange(8)
    r = rl + (rl >= c)
    valid = (rl < 7)
    t[:, C_EWF:C_EWF + 8] = (1024 * np.maximum(c - 1 - r, 0) * valid)[None, :]
    t[:, C_MWF:C_MWF + 8] = ((r < c) & valid)[None, :]
    t[:, C_EWB:C_EWB + 8] = (1024 * np.maximum(r - c - 1, 0) * valid)[None, :]
    t[:, C_MWB:C_MWB + 8] = ((r > c) & valid)[None, :]
    t[:, C_IOTA:C_IOTA + 16] = np.arange(16)[None, :]
    t[:, C_ID:C_ID + 128] = np.eye(128)
    return t


_NC_CACHE = {}


def _prep(x_prompt, x_sample, w_in, ret_log_rate_fwd, ret_log_rate_bwd, ret_gn_g, conv_w, w_out,
          ln1_g, ln1_b, peer_w_query, peer_keys_1, peer_keys_2, peer_u, peer_v, ln2_g, ln2_b, cores=range(8)):
    f = np.float32
    xs = [np.asarray(x_prompt[0], f), np.asarray(x_sample[0], f), np.asarray(x_sample[1], f)]
    xTfull = [np.ascontiguousarray(x.T) for x in xs]
    inv = (np.float32(10000.0) ** (-np.arange(0, 128, 2, dtype=np.float32) / np.float32(128))).astype(np.float32)
    ang = (np.arange(8192, dtype=np.float32)[:, None] * inv[None, :]).astype(np.float32)
    cosT = np.cos(ang.astype(np.float64)).astype(f)
    sinT = np.sin(ang.astype(np.float64)).astype(f)
    cs = np.stack([cosT, sinT])
    csall = np.ascontiguousarray(cs.reshape(2, 64, 128, 64).transpose(2, 0, 1, 3))
    keys = np.stack([np.asarray(peer_keys_1[0], f), np.asarray(peer_keys_2[0], f)], axis=1)
    keysT = np.ascontiguousarray(keys.reshape(16, 128, 128).transpose(2, 0, 1))
    rep = lambda v: np.ascontiguousarray(np.broadcast_to(np.asarray(v, f).reshape(1, -1), (128, np.asarray(v).size)))
    lnv = np.stack([rep(ln1_g[0]), rep(ln1_b[0]), rep(ln2_g[0]), rep(ln2_b[0])])
    gng = rep(ret_gn_g[0])
    cwT = np.ascontiguousarray(np.asarray(conv_w[0], f).T.reshape(8, 128, 3).transpose(1, 0, 2))
    rates = np.concatenate([np.asarray(ret_log_rate_fwd[0], f), np.asarray(ret_log_rate_bwd[0], f)])
    w_in0 = np.ascontiguousarray(np.asarray(w_in[0], f))
    w_out0 = np.ascontiguousarray(np.asarray(w_out[0], f))
    w_q0 = np.ascontiguousarray(np.asarray(peer_w_query[0], f))
    pu0 = np.ascontiguousarray(np.asarray(peer_u[0], f))
    pv0 = np.ascontiguousarray(np.asarray(peer_v[0], f))
    in_maps = []
    for c in cores:
        lo = 1024 * c
        xtok = np.ascontiguousarray(np.stack([x[lo:lo + 1024] for x in xs]))
        xTs = np.zeros((3, 2048, 1024), f)
        xTh = np.zeros((3, 2048, 64), f)
        for s in range(3):
            xTs[s] = xs[s][lo:lo + 1024].T
            if lo - 1 >= 0:
                xTh[s, :, 0] = xs[s][lo - 1]
            if lo + 1024 < 8192:
                xTh[s, :, 1] = xs[s][lo + 1024]
        xTall = np.ascontiguousarray(np.stack([np.concatenate([xt_[:, :lo], xt_[:, lo + 1024:]], axis=1) for xt_ in xTfull]))
        oth = [g_ for g_ in range(64) if not (8 * c <= g_ < 8 * c + 8)]
        csoth = np.ascontiguousarray(csall[:, :, oth, :])
        ct = _host_tables(c)
        ct[:, C_RATE:C_RATE + 16] = rates[None, :]
        csown = np.ascontiguousarray(csall[:, :, 8 * c:8 * c + 8, :])
        in_maps.append(dict(xtok=xtok, xTs=xTs, xTh=xTh, xTall=xTall, w_in=w_in0, w_out=w_out0, w_q=w_q0, keysT=keysT,
                            pu=pu0, pv=pv0, ctab=ct, csall=csoth, csown=csown, gng=gng, cwT=cwT, lnv=lnv))
    return in_maps


def kernel(**inputs):
    f = np.float32
    in_maps = _prep(**inputs)
    if 'nc' not in _NC_CACHE:
        _NC_CACHE['nc'] = build()
    res = run_bass_kernel_spmd(_NC_CACHE['nc'], in_maps, core_ids=list(range(8)))
    ys = [np.zeros((8192, 2048), f) for _ in range(3)]
    for c in range(8):
        yc = res.results[c]["y"]
        for s in range(3):
            ys[s][1024 * c:1024 * (c + 1)] = yc[s]
    return (ys[0][None], np.stack([ys[1], ys[2]]))
```

```python
import numpy as np
from contextlib import ExitStack
import concourse.bass as bass
import concourse.mybir as mybir
from concourse.bass_utils import run_bass_kernel_spmd

F32 = mybir.dt.float32
BF16 = mybir.dt.bfloat16
I32 = mybir.dt.int32
U32 = mybir.dt.uint32
AF = mybir.ActivationFunctionType
ALU = mybir.AluOpType
AX = mybir.AxisListType

ALPHA = 2.0 ** 0.25
LN_EPS = 1e-5
GN_EPS = 1e-6
SK = 128.0 ** -0.5
NSEG = 3
NBLK = 7
NCH = 8

C_RATE = 0
C_EPF, C_MPF, C_EPB, C_MPB = 16, 144, 272, 400
C_EXF, C_EXB = 528, 656
C_EZF1, C_EZB1 = 784, 785
C_EZFA, C_EZBA = 786, 794
C_EWF, C_MWF, C_EWB, C_MWB = 802, 810, 818, 826
C_IOTA = 834
C_ID = 850
NT = 978


class Buf:
    def __init__(self, t):
        self.t = t
        self.w = {}
        self.r = {}
        self.dkey = None
        self.dcnt = 0
        self.psum = False

    def __getitem__(self, k):
        return self.t[k]


class Prog:
    def __init__(self, nc, es):
        self.nc = nc
        self.es = es
        self.names = ['pe', 'dve', 'act', 'pool', 'sp']
        self.semh = {}
        for k in self.names:
            self.semh[k] = es.enter_context(nc.semaphore('s_' + k))
        self.cnt = {k: 0 for k in self.names}
        self.waited = {k: {} for k in self.names}
        self.q = {k: [] for k in self.names}
        self.nd = 0
        self.dbufs = []

    def _need(self, reads, writes, e=None):
        need = {}
        for b in reads:
            for k, v in b.w.items():
                if need.get(k, 0) < v:
                    need[k] = v
            if b.psum:
                for k, v in b.r.items():
                    if k != e and need.get(k, 0) < v:
                        need[k] = v
        for b in writes:
            for k, v in b.w.items():
                if need.get(k, 0) < v:
                    need[k] = v
            for k, v in b.r.items():
                if need.get(k, 0) < v:
                    need[k] = v
        return need

    def _waits(self, e, need):
        waits = []
        wd = self.waited[e]
        for k, v in need.items():
            if k == e and e == 'pe':
                continue
            if wd.get(k, 0) < v:
                wd[k] = v
                waits.append((self.semh[k], v))
        return waits

    def op(self, e, name, R=(), W=(), **kw):
        waits = self._waits(e, self._need(R, W, e))
        self.cnt[e] += 1
        tok = self.cnt[e]
        self.q[e].append((waits, name, kw, self.semh[e], 1))
        for b in R:
            if b.r.get(e, 0) < tok:
                b.r[e] = tok
        for b in W:
            b.w[e] = tok

    def dma(self, e, sb, out, in_, R=(), W=(), name='dma_start', nodep=False, **kw):
        if sb.dkey is None:
            sb.dkey = 'd%d' % self.nd
            self.nd += 1
            self.semh[sb.dkey] = self.es.enter_context(self.nc.semaphore(sb.dkey))
            self.dbufs.append(sb)
        waits = [] if nodep else self._waits(e, self._need(R, W))
        sb.dcnt += 16
        kw = dict(kw)
        kw['out'] = out
        kw['in_'] = in_
        self.q[e].append((waits, name, kw, self.semh[sb.dkey], 16))
        for b in R:
            if b.r.get(sb.dkey, 0) < sb.dcnt:
                b.r[sb.dkey] = sb.dcnt
        for b in W:
            b.w[sb.dkey] = sb.dcnt

    def barrier(self):
        need = {k: self.cnt[k] for k in self.names if self.cnt[k] > 0}
        for b in self.dbufs:
            need[b.dkey] = b.dcnt
        for e in self.names:
            waits = self._waits(e, need)
            if waits:
                self.q[e].append((waits, None, None, None, 0))

    def replay(self, e, E):
        for waits, name, kw, sem, inc in self.q[e]:
            for s, v in waits:
                E.wait_ge(s, v)
            if name is None:
                continue
            ins = getattr(E, name)(**kw)
            ins.then_inc(sem, inc)


def build(dbg=None):
    nc = bass.Bass("TRN2", target_bir_lowering=False)
    dbg = dbg or {}
    A_BLOCKS = dbg.get('a_blocks', [(s_, r_) for s_ in range(NSEG) for r_ in range(NBLK)])
    SEGS = dbg.get('segs', list(range(NSEG)))
    HEADS = dbg.get('heads', list(range(8)))
    GROUPS = dbg.get('groups', list(range(8)))
    TILES = dbg.get('tiles', list(range(NCH)))
    NTOK = dbg.get('ntok', 128)
    HST = dbg.get('hstage', 99)
    NUV = 6
    dumps = []

    def din(name, shape, dt=F32):
        return nc.dram_tensor(name, shape, dt, kind="ExternalInput").ap()

    xtok = din("xtok", [NSEG, 1024, 2048])
    xTs = din("xTs", [NSEG, 2048, 1024])
    xTh = din("xTh", [NSEG, 2048, 64])
    xTall = din("xTall", [NSEG, 2048, 1024 * NBLK])
    w_in = din("w_in", [2048, 7168])
    w_out = din("w_out", [2048, 2048])
    w_q = din("w_q", [2048, 2048])
    keysT = din("keysT", [128, 16, 128])
    pu = din("pu", [16384, 2048])
    pv = din("pv", [16384, 2048])
    ctab = din("ctab", [128, NT])
    csall = din("csall", [128, 2, 8 * NBLK, 64])
    csown = din("csown", [128, 2, 8, 64])
    gng = din("gng", [128, 1024])
    cwT = din("cwT", [128, 8, 3])
    lnv = din("lnv", [4, 128, 2048])
    y = nc.dram_tensor("y", [NSEG, 1024, 2048], F32, kind="ExternalOutput").ap()
    sin_d = nc.dram_tensor("sin_d", [NSEG, 2, 128, 1024], F32, kind=("ExternalInput" if dbg.get('sin_in') else "Internal")).ap()
    cat_in = din("cat_in", [128, 16, 1024]) if dbg.get('cat_in') else None
    uv16 = nc.dram_tensor("uv16", [16384, 4096], BF16, kind="Internal").ap()
    w_in16 = nc.dram_tensor("w_in16", [2048, 7168], BF16, kind="Internal").ap()
    w_out16 = nc.dram_tensor("w_out16", [2048, 2048], BF16, kind="Internal").ap()
    w_q16 = nc.dram_tensor("w_q16", [2048, 2048], BF16, kind="Internal").ap()

    es = ExitStack()
    with es:
        P = Prog(nc, es)

        uid = [0]

        def SB(name, shape, dt=F32, stack=es):
            uid[0] += 1
            return Buf(stack.enter_context(nc.sbuf_tensor("%s_%d" % (name, uid[0]), shape, dt)))

        def PS(name, shape, dt=F32, stack=es):
            uid[0] += 1
            b = Buf(stack.enter_context(nc.psum_tensor("%s_%d" % (name, uid[0]), shape, dt)))
            b.psum = True
            return b

        def dump(name, buf, ap, shape, dt=F32):
            if name not in dbg.get('dump', []):
                return
            d = nc.dram_tensor("dbg_" + name, shape, dt, kind="ExternalOutput").ap()
            P.dma('sp', buf, d, ap, R=[buf])

        ct = SB("ct", [128, NT])
        idb = SB("idb", [128, 128], BF16)
        lg = SB("lg", [128, 16])
        DmT = SB("DmT", [128, 8, 128])
        XiF = SB("XiF", [128, 8, 128])
        XiB = SB("XiB", [128, 8, 128])
        Z1 = SB("Z1", [128, 16])
        ZA = SB("ZA", [128, 2, 8, 8])
        G128 = SB("G128", [128, 16])
        Wc = SB("Wc", [128, 2, 8, 8])
        cso = SB("cso", [128, 2, 8, 64])
        gn = SB("gn", [128, 1024])
        cw = SB("cw", [128, 8, 3])
        kT16 = SB("kT16", [128, 16, 128], BF16)
        tA = SB("tA", [128, 128])
        tB = SB("tB", [128, 128])

        P.dma('sp', ct, ct[:], ctab, W=[ct])
        P.dma('pool', idb, idb[:], ctab[:, C_ID:C_ID + 128], W=[idb])
        P.dma('sp', cso, cso[:], csown, W=[cso])
        P.dma('sp', gn, gn[:], gng, W=[gn])
        P.dma('sp', cw, cw[:], cwT, W=[cw])
        P.dma('pool', kT16, kT16[:], keysT, W=[kT16])

        P.op('act', 'activation', R=[ct], W=[lg], out=lg[:], in_=ct[:, C_RATE:C_RATE + 16], func=AF.Exp)
        P.op('dve', 'tensor_scalar', R=[lg], W=[lg], out=lg[:], in0=lg[:], scalar1=-1.0, scalar2=None, op0=ALU.mult)
        for h in range(8):
            lf = lg[:, h:h + 1]
            lb = lg[:, 8 + h:9 + h]
            P.op('act', 'activation', R=[ct, lg], W=[tA], out=tA[:], in_=ct[:, C_EPF:C_EPF + 128], func=AF.Exp, scale=lf)
            P.op('dve', 'tensor_tensor', R=[tA, ct], W=[tA], out=tA[:], in0=tA[:], in1=ct[:, C_MPF:C_MPF + 128], op=ALU.mult)
            P.op('act', 'activation', R=[ct, lg], W=[tB], out=tB[:], in_=ct[:, C_EPB:C_EPB + 128], func=AF.Exp, scale=lb)
            P.op('dve', 'tensor_tensor', R=[tB, ct], W=[tB], out=tB[:], in0=tB[:], in1=ct[:, C_MPB:C_MPB + 128], op=ALU.mult)
            P.op('dve', 'tensor_tensor', R=[tA, tB], W=[DmT], out=DmT[:, h, :], in0=tA[:], in1=tB[:], op=ALU.add)
            P.op('act', 'activation', R=[ct, lg], W=[XiF], out=XiF[:, h, :], in_=ct[:, C_EXF:C_EXF + 128], func=AF.Exp, scale=lf)
            P.op('act', 'activation', R=[ct, lg], W=[XiB], out=XiB[:, h, :], in_=ct[:, C_EXB:C_EXB + 128], func=AF.Exp, scale=lb)
            P.op('act', 'activation', R=[ct, lg], W=[ZA], out=ZA[:, 0, :, h], in_=ct[:, C_EZFA:C_EZFA + 8], func=AF.Exp, scale=lf)
            P.op('act', 'activation', R=[ct, lg], W=[ZA], out=ZA[:, 1, :, h], in_=ct[:, C_EZBA:C_EZBA + 8], func=AF.Exp, scale=lb)
            P.op('act', 'activation', R=[ct, lg], W=[Wc], out=Wc[:, 0, :, h], in_=ct[:, C_EWF:C_EWF + 8], func=AF.Exp, scale=lf)
            P.op('act', 'activation', R=[ct, lg], W=[Wc], out=Wc[:, 1, :, h], in_=ct[:, C_EWB:C_EWB + 8], func=AF.Exp, scale=lb)
        P.op('dve', 'tensor_scalar', R=[DmT], W=[DmT], out=DmT[:], in0=DmT[:], scalar1=SK, scalar2=None, op0=ALU.mult)
        P.op('dve', 'tensor_scalar', R=[ZA], W=[ZA], out=ZA[:], in0=ZA[:], scalar1=SK, scalar2=None, op0=ALU.mult)
        P.op('dve', 'tensor_tensor', R=[Wc, ct], W=[Wc], out=Wc[:, 0], in0=Wc[:, 0],
             in1=ct[:, C_MWF:C_MWF + 8].unsqueeze(2).to_broadcast([128, 8, 8]), op=ALU.mult)
        P.op('dve', 'tensor_tensor', R=[Wc, ct], W=[Wc], out=Wc[:, 1], in0=Wc[:, 1],
             in1=ct[:, C_MWB:C_MWB + 8].unsqueeze(2).to_broadcast([128, 8, 8]), op=ALU.mult)
        P.op('dve', 'tensor_scalar', R=[lg, ct], W=[Z1], out=Z1[:, 0:8], in0=lg[:, 0:8], scalar1=ct[:, C_EZF1:C_EZF1 + 1], scalar2=None, op0=ALU.mult)
        P.op('dve', 'tensor_scalar', R=[lg, ct], W=[Z1], out=Z1[:, 8:16], in0=lg[:, 8:16], scalar1=ct[:, C_EZB1:C_EZB1 + 1], scalar2=None, op0=ALU.mult)
        P.op('act', 'activation', R=[Z1], W=[Z1], out=Z1[:], in_=Z1[:], func=AF.Exp)
        P.op('dve', 'tensor_scalar', R=[Z1], W=[Z1], out=Z1[:], in0=Z1[:], scalar1=SK, scalar2=None, op0=ALU.mult)
        P.op('act', 'activation', R=[lg], W=[G128], out=G128[:], in_=lg[:], func=AF.Exp, scale=128.0)

        dump('DmT', DmT, DmT[:], [128, 8, 128])
        dump('XiF', XiF, XiF[:], [128, 8, 128])
        dump('XiB', XiB, XiB[:], [128, 8, 128])
        dump('Z1', Z1, Z1[:], [128, 16])
        dump('ZA', ZA, ZA[:], [128, 2, 8, 8])
        dump('Wc', Wc, Wc[:], [128, 2, 8, 8])
        dump('G128', G128, G128[:], [128, 16])

        UV16 = Buf(uv16)
        CONV_ROWS = 256
        W16 = Buf(w_in16)
        conv_jobs = [(W16, w_in16, w_in, i, 128) for i in range(16)] + [(W16, w_out16, w_out, i, 256) for i in range(8)] + \
                    [(W16, w_q16, w_q, i, 256) for i in range(8)] + \
                    [(UV16, uv16[:, 0:2048], pu, i, CONV_ROWS) for i in range(16384 // CONV_ROWS)] + \
                    [(UV16, uv16[:, 2048:4096], pv, i, CONV_ROWS) for i in range(16384 // CONV_ROWS)]
        conv_jobs.reverse()

        def conv_step(nmax=1):
            for _ in range(nmax):
                if not conv_jobs:
                    return
                B_, dst_, src_, i_, nr = conv_jobs.pop()
                P.dma('pool', B_, dst_[i_ * nr:(i_ + 1) * nr, :], src_[i_ * nr:(i_ + 1) * nr, :], W=[B_], nodep=True)

        def rotary(src, dst, cos, sin, H, t1, t2, t3, t4, srcB, dstB, csB):
            cb = cos.unsqueeze(1).to_broadcast([128, H, 64])
            sb_ = sin.unsqueeze(1).to_broadcast([128, H, 64])
            x1 = src[:, :, 0:64]
            x2 = src[:, :, 64:128]
            P.op('dve', 'tensor_tensor', R=[srcB, csB], W=[t1], out=t1[:, 0:H, :], in0=x1, in1=cb, op=ALU.mult)
            P.op('dve', 'tensor_tensor', R=[srcB, csB], W=[t2], out=t2[:, 0:H, :], in0=x2, in1=sb_, op=ALU.mult)
            P.op('dve', 'tensor_tensor', R=[t1, t2], W=[dstB], out=dst[:, :, 0:64], in0=t1[:, 0:H, :], in1=t2[:, 0:H, :], op=ALU.subtract)
            P.op('pool', 'tensor_tensor', R=[srcB, csB], W=[t3], out=t3[:, 0:H, :], in0=x1, in1=sb_, op=ALU.mult)
            P.op('pool', 'tensor_tensor', R=[srcB, csB], W=[t4], out=t4[:, 0:H, :], in0=x2, in1=cb, op=ALU.mult)
            P.op('pool', 'tensor_tensor', R=[t3, t4], W=[dstB], out=dst[:, :, 64:128], in0=t3[:, 0:H, :], in1=t4[:, 0:H, :], op=ALU.add)


        with ExitStack() as sa:
            Wkv = SB("Wkv", [128, 16, 2048], BF16, sa)
            rt = [SB("rtA%d" % i, [128, 8, 64], F32, sa) for i in range(4)]
            csA = SB("csA", [128, 2, 8 * NBLK, 64], F32, sa)
            xa = [SB("xa%d" % i, [128, 16, 128], BF16, sa) for i in range(2)]
            ksb = SB("ksb", [128, 8, 128], F32, sa)
            kr = SB("kr", [128, 8, 128], F32, sa)
            kfA = SB("kfA", [128, 8, 128], BF16, sa)
            kbA = SB("kbA", [128, 8, 128], BF16, sa)
            vA = SB("vA", [128, 1024], BF16, sa)
            SinAcc = [[SB("sin%d_%d" % (s, d), [128, 1024], F32, sa) for d in range(2)] for s in range(NSEG)]
            pk = PS("pk", [128, 1024], F32, sa)
            pvv = PS("pvv", [128, 1024], F32, sa)
            pAf = PS("pAf", [128, 1024], F32, sa)
            pAb = PS("pAb", [128, 1024], F32, sa)
            w_kv = w_in[:, 1024:3072].rearrange("(c p) n -> p c n", p=128)
            for c4 in range(4):
                P.dma('pool', Wkv, Wkv[:, 4 * c4:4 * c4 + 4, :], w_kv[:, 4 * c4:4 * c4 + 4, :], W=[Wkv])
            P.dma('sp', csA, csA[:], csall, W=[csA])
            for s in range(NSEG):
                for d in range(2):
                    P.op('pool', 'memset', W=[SinAcc[s][d]], ap=SinAcc[s][d][:], constant=0.0)
            it = 0
            for s in range(NSEG):
                for r in range(NBLK):
                    if (s, r) not in A_BLOCKS:
                        continue
                    for m in range(NCH):
                        g = 8 * r + m
                        xb_ = xa[it % 2]
                        it += 1
                        P.dma('pool', xb_, xb_[:], xTall[s, :, g * 128:(g + 1) * 128].rearrange("(c p) t -> p c t", p=128), W=[xb_])
                        conv_step(1)
                        for half in range(2):
                            for c in range(16):
                                P.op('pe', 'matmul', R=[xb_, Wkv], W=[pk], out=pk[:, half * 512:(half + 1) * 512], lhsT=xb_[:, c, :],
                                     rhs=Wkv[:, c, half * 512:(half + 1) * 512], start=(c == 0), stop=(c == 15))
                        for half in range(2):
                            for c in range(16):
                                P.op('pe', 'matmul', R=[xb_, Wkv], W=[pvv], out=pvv[:, half * 512:(half + 1) * 512], lhsT=xb_[:, c, :],
                                     rhs=Wkv[:, c, 1024 + half * 512:1024 + (half + 1) * 512], start=(c == 0), stop=(c == 15))
                        P.op('act', 'activation', R=[pk], W=[ksb], out=ksb[:].rearrange("p h d -> p (h d)"), in_=pk[:], func=AF.Copy)
                        P.op('act', 'activation', R=[pvv], W=[vA], out=vA[:], in_=pvv[:], func=AF.Copy)
                        rotary(ksb[:], kr[:], csA[:, 0, g, :], csA[:, 1, g, :], 8, rt[0], rt[1], rt[2], rt[3], ksb, kr, csA)
                        P.op('dve', 'tensor_tensor', R=[kr, ZA], W=[kfA], out=kfA[:], in0=kr[:],
                             in1=ZA[:, 0, m, :].unsqueeze(2).to_broadcast([128, 8, 128]), op=ALU.mult)
                        P.op('pool', 'tensor_tensor', R=[kr, ZA], W=[kbA], out=kbA[:], in0=kr[:],
                             in1=ZA[:, 1, m, :].unsqueeze(2).to_broadcast([128, 8, 128]), op=ALU.mult)
                        for h in range(8):
                            P.op('pe', 'matmul', R=[kfA, vA], W=[pAf], out=pAf[:, h * 128:(h + 1) * 128], lhsT=kfA[:, h, :],
                                 rhs=vA[:, h * 128:(h + 1) * 128], start=(m == 0 and h % 4 == 0), stop=(m == 7), skip_group_check=True)
                            P.op('pe', 'matmul', R=[kbA, vA], W=[pAb], out=pAb[:, h * 128:(h + 1) * 128], lhsT=kbA[:, h, :],
                                 rhs=vA[:, h * 128:(h + 1) * 128], start=(m == 0 and h % 4 == 0), stop=(m == 7), skip_group_check=True)
                    for d, pA in ((0, pAf), (1, pAb)):
                        acc = SinAcc[s][d]
                        for h in range(8):
                            P.op('dve', 'scalar_tensor_tensor', R=[pA, Wc, acc], W=[acc], out=acc[:, h * 128:(h + 1) * 128],
                                 in0=pA[:, h * 128:(h + 1) * 128], scalar=Wc[:, d, r, h:h + 1], in1=acc[:, h * 128:(h + 1) * 128],
                                 op0=ALU.mult, op1=ALU.add)
            dump('kr', kr, kr[:], [128, 8, 128])
            dump('sin00', SinAcc[0][0], SinAcc[0][0][:], [128, 1024])
            dump('sin01', SinAcc[0][1], SinAcc[0][1][:], [128, 1024])
            for s in range(NSEG):
                for d in range(2):
                    if not dbg.get('sin_in'):
                        P.dma('sp', SinAcc[s][d], sin_d[s, d], SinAcc[s][d][:], R=[SinAcc[s][d]])
            conv_step(10 ** 6)
            P.barrier()

        catT = SB("catT", [128, 16, 1024], BF16)
        WB = [SB("WB%d" % i, [128, 16, 512], BF16) for i in range(2)]
        wbi = [0]

        def load_wb(src_cols):
            b = WB[wbi[0] % 2]
            wbi[0] += 1
            for apx, off, n in src_cols:
                P.dma('sp', b, b[:, :, off:off + n], apx.rearrange("(c p) n -> p c n", p=128), R=[W16], W=[b])
            return b

        pM = PS("pM", [128, 2048], F32)
        pX0 = PS("pX0", [128, 512], F32)
        pX1 = PS("pX1", [128, 512], F32)
        pX2 = PS("pX2", [128, 512], F32)
        pT = PS("pT", [128, 1024], BF16)

        for s in SEGS:
            with ExitStack() as s1:
                xT = SB("xT", [128, 16, 1024], BF16, s1)
                xH = SB("xH", [128, 16, 64], BF16, s1)
                rt = [SB("rtB%d" % i, [128, 8, 64], F32, s1) for i in range(4)]
                for c4 in range(4):
                    P.dma('pool', xT, xT[:, 4 * c4:4 * c4 + 4, :],
                          xTs[s].rearrange("(c p) t -> p c t", p=128)[:, 4 * c4:4 * c4 + 4, :], W=[xT])
                P.dma('pool', xH, xH[:], xTh[s].rearrange("(c p) t -> p c t", p=128), W=[xH])
                QT = SB("QT", [128, 8, 128], BF16, s1)
                QTF = SB("QTF", [128, 8, 128], BF16, s1)
                QTB = SB("QTB", [128, 8, 128], BF16, s1)
                KT = SB("KT", [128, 8, 128], BF16, s1)
                KF = SB("KF", [128, 8, 128], BF16, s1)
                KB = SB("KB", [128, 8, 128], BF16, s1)
                VB = SB("VB", [128, 8, 128], BF16, s1)
                GS = SB("GS", [128, 8, 128], F32, s1)
                qkA = SB("qkA", [128, 8, 2, 128], F32, s1)
                qkR = SB("qkR", [128, 8, 2, 128], F32, s1)
                qkB = SB("qkB", [128, 8, 2, 128], BF16, s1)
                Oall = SB("Oall", [128, 8, 128], F32, s1)
                RO = SB("RO", [128, 8, 128], BF16, s1)
                SbN = [SB("SbN%d" % n, [128, 128], BF16, s1) for n in range(NCH)]
                SfN = SB("SfN", [128, 128], BF16, s1)
                Sf = SB("Sf", [128, 128], F32, s1)
                Sb = SB("Sb", [128, 128], F32, s1)
                scm = SB("scm", [128, 128], BF16, s1)
                st1 = SB("st1", [128, 4, 8], F32, s1)
                hcs = SB("hcs", [128, 1026], F32, s1)
                us = SB("us", [128, 1026], F32, s1)
                bgs = SB("bgs", [128, 1024], F32, s1)
                cv = SB("cv", [128, 1024], F32, s1)

                for h in HEADS:
                    wb = load_wb([(w_in16[:, j * 1024 + h * 128: j * 1024 + (h + 1) * 128], j * 128, 128) for j in range(4)])
                    hs = slice(h * 128, (h + 1) * 128)
                    P.dma('sp', Sf, Sf[:], sin_d[s, 0][:, hs], W=[Sf])
                    P.dma('sp', Sb, Sb[:], sin_d[s, 1][:, hs], W=[Sb])
                    for n in range(NCH):
                        pp = pX0 if n % 2 == 0 else pX2
                        for c in range(16):
                            P.op('pe', 'matmul', R=[xT, wb], W=[pp], out=pp[:], lhsT=xT[:, c, 128 * n:128 * (n + 1)],
                                 rhs=wb[:, c, :], start=(c == 0), stop=(c == 15))
                        P.op('act', 'activation', R=[pp], W=[qkA], out=qkA[:, n].rearrange("p a d -> p (a d)"), in_=pp[:, 0:256], func=AF.Copy)
                        P.op('act', 'activation', R=[pp], W=[VB], out=VB[:, n, :], in_=pp[:, 256:384], func=AF.Copy)
                        P.op('act', 'activation', R=[pp], W=[GS], out=GS[:, n, :], in_=pp[:, 384:512], func=AF.Silu)
                    if HST < 1:
                        continue
                    for r4 in range(2):
                        cs_ = slice(4 * r4, 4 * r4 + 4)
                        src = qkA[:, cs_]
                        dst = qkR[:, cs_]
                        cb = cso[:, 0, cs_, :].unsqueeze(2).to_broadcast([128, 4, 2, 64])
                        sb_ = cso[:, 1, cs_, :].unsqueeze(2).to_broadcast([128, 4, 2, 64])
                        x1 = src[:, :, :, 0:64]
                        x2 = src[:, :, :, 64:128]
                        tv = [t[:].rearrange("p (a b) d -> p a b d", b=2) for t in rt]
                        P.op('dve', 'tensor_tensor', R=[qkA, cso], W=[rt[0]], out=tv[0], in0=x1, in1=cb, op=ALU.mult)
                        P.op('dve', 'tensor_tensor', R=[qkA, cso], W=[rt[1]], out=tv[1], in0=x2, in1=sb_, op=ALU.mult)
                        P.op('dve', 'tensor_tensor', R=[rt[0], rt[1]], W=[qkR], out=dst[:, :, :, 0:64], in0=tv[0], in1=tv[1], op=ALU.subtract)
                        P.op('pool', 'tensor_tensor', R=[qkA, cso], W=[rt[2]], out=tv[2], in0=x1, in1=sb_, op=ALU.mult)
                        P.op('pool', 'tensor_tensor', R=[qkA, cso], W=[rt[3]], out=tv[3], in0=x2, in1=cb, op=ALU.mult)
                        P.op('pool', 'tensor_tensor', R=[rt[2], rt[3]], W=[qkR], out=dst[:, :, :, 64:128], in0=tv[2], in1=tv[3], op=ALU.add)
                    if HST < 2:
                        continue
                    P.op('dve', 'tensor_copy', R=[qkR], W=[qkB], out=qkB[:], in_=qkR[:])
                    P.op('dve', 'tensor_scalar', R=[qkR, Z1], W=[KF], out=KF[:], in0=qkR[:, :, 1, :], scalar1=Z1[:, h:h + 1], scalar2=None, op0=ALU.mult)
                    P.op('pool', 'tensor_scalar', R=[qkR, Z1], W=[KB], out=KB[:], in0=qkR[:, :, 1, :], scalar1=Z1[:, 8 + h:9 + h], scalar2=None, op0=ALU.mult)
                    if HST < 3:
                        continue
                    for r4 in range(2):
                        cs_ = slice(4 * r4, 4 * r4 + 4)
                        for n in range(4 * r4, 4 * r4 + 4):
                            for a_ in range(2):
                                o0 = (n % 4) * 256 + a_ * 128
                                P.op('pe', 'transpose', R=[qkB, idb], W=[pT], out=pT[:, o0:o0 + 128], in_=qkB[:, n, a_, :], identity=idb[:])
                        pT4 = pT[:].rearrange("p (n a d) -> p n a d", n=4, a=2)
                        P.op('act', 'activation', R=[pT], W=[QT], out=QT[:, cs_, :], in_=pT4[:, :, 0, :], func=AF.Copy)
                        P.op('dve', 'tensor_tensor', R=[pT, XiF], W=[QTF], out=QTF[:, cs_, :], in0=pT4[:, :, 0, :],
                             in1=XiF[:, h, :].unsqueeze(1).to_broadcast([128, 4, 128]), op=ALU.mult)
                        P.op('dve', 'tensor_tensor', R=[pT, XiB], W=[QTB], out=QTB[:, cs_, :], in0=pT4[:, :, 0, :],
                             in1=XiB[:, h, :].unsqueeze(1).to_broadcast([128, 4, 128]), op=ALU.mult)
                        P.op('act', 'activation', R=[pT], W=[KT], out=KT[:, cs_, :], in_=pT4[:, :, 1, :], func=AF.Copy)
                    if HST < 4:
                        continue
                    for n in range(NCH - 1, -1, -1):
                        P.op('act', 'activation', R=[Sb], W=[SbN[n]], out=SbN[n][:], in_=Sb[:], func=AF.Copy)
                        if n > 0:
                            P.op('pe', 'matmul', R=[KB, VB], W=[pX2], out=pX2[:, 0:128], lhsT=KB[:, n, :], rhs=VB[:, n, :], start=True, stop=True)
                            P.op('dve', 'scalar_tensor_tensor', R=[Sb, G128, pX2], W=[Sb], out=Sb[:], in0=Sb[:], scalar=G128[:, 8 + h:9 + h],
                                 in1=pX2[:, 0:128], op0=ALU.mult, op1=ALU.add)
                    for n in range(NCH):
                        P.op('act', 'activation', R=[Sf], W=[SfN], out=SfN[:], in_=Sf[:], func=AF.Copy)
                        P.op('pe', 'matmul', R=[KT, QT], W=[pX1], out=pX1[:, 0:128], lhsT=KT[:, n, :], rhs=QT[:, n, :], start=True, stop=True)
                        P.op('dve', 'tensor_tensor', R=[pX1, DmT], W=[scm], out=scm[:], in0=pX1[:, 0:128], in1=DmT[:, h, :], op=ALU.mult)
                        P.op('pe', 'matmul', R=[scm, VB], W=[pX1], out=pX1[:, 128:256], lhsT=scm[:], rhs=VB[:, n, :], start=True, stop=False)
                        P.op('pe', 'matmul', R=[QTF, SfN], W=[pX1], out=pX1[:, 128:256], lhsT=QTF[:, n, :], rhs=SfN[:], start=False, stop=False)
                        P.op('pe', 'matmul', R=[QTB, SbN[n]], W=[pX1], out=pX1[:, 128:256], lhsT=QTB[:, n, :], rhs=SbN[n][:], start=False, stop=True)
                        if n < NCH - 1:
                            P.op('pe', 'matmul', R=[KF, VB], W=[pX2], out=pX2[:, 128:256], lhsT=KF[:, n, :], rhs=VB[:, n, :], start=True, stop=True)
                            P.op('dve', 'scalar_tensor_tensor', R=[Sf, G128, pX2], W=[Sf], out=Sf[:], in0=Sf[:], scalar=G128[:, h:h + 1],
                                 in1=pX2[:, 128:256], op0=ALU.mult, op1=ALU.add)
                        P.op('act', 'activation', R=[pX1], W=[Oall], out=Oall[:, n, :], in_=pX1[:, 128:256], func=AF.Copy)
                    if HST < 5:
                        continue
                    sq = qkA[:].rearrange("p n a d -> p (n a d)")[:, 0:1024].rearrange("p (n d) -> p n d", n=8)
                    P.op('dve', 'tensor_reduce', R=[Oall], W=[st1], out=st1[:, 0, :], in_=Oall[:], axis=AX.X, op=ALU.add)
                    P.op('dve', 'tensor_scalar', R=[st1], W=[st1], out=st1[:, 1, :], in0=st1[:, 0, :], scalar1=-1.0 / 128, scalar2=None, op0=ALU.mult)
                    P.op('dve', 'tensor_tensor', R=[Oall, st1], W=[Oall], out=Oall[:], in0=Oall[:], in1=st1[:, 1, :].unsqueeze(2).to_broadcast([128, 8, 128]), op=ALU.add)
                    if HST < 5.2:
                        continue
                    P.op('pool', 'tensor_tensor', R=[Oall], W=[qkA], out=sq, in0=Oall[:], in1=Oall[:], op=ALU.mult)
                    P.op('dve', 'tensor_reduce', R=[qkA], W=[st1], out=st1[:, 2, :], in_=sq, axis=AX.X, op=ALU.add)
                    P.op('dve', 'tensor_scalar', R=[st1], W=[st1], out=st1[:, 3, :], in0=st1[:, 2, :], scalar1=1.0 / 128, scalar2=GN_EPS, op0=ALU.mult, op1=ALU.add)
                    if HST < 5.3:
                        continue
                    P.op('act', 'activation', R=[st1], W=[st1], out=st1[:, 3, :], in_=st1[:, 3, :], func=AF.Sqrt)
                    P.op('dve', 'reciprocal', R=[st1], W=[st1], out=st1[:, 3, :], in_=st1[:, 3, :])
                    if HST < 5.4:
                        continue
                    P.op('pool', 'tensor_tensor', R=[GS, gn], W=[GS], out=GS[:], in0=GS[:], in1=gn[:, hs].unsqueeze(1).to_broadcast([128, 8, 128]), op=ALU.mult)
                    P.op('dve', 'tensor_tensor', R=[Oall, st1], W=[Oall], out=Oall[:], in0=Oall[:], in1=st1[:, 3, :].unsqueeze(2).to_broadcast([128, 8, 128]), op=ALU.mult)
                    P.op('dve', 'tensor_tensor', R=[Oall, GS], W=[RO], out=RO[:], in0=Oall[:], in1=GS[:], op=ALU.mult)
                    if HST < 5.5:
                        continue
                    for n in range(NCH):
                        P.op('pe', 'transpose', R=[RO, idb], W=[pT], out=pT[:, n * 128:(n + 1) * 128], in_=RO[:, n, :], identity=idb[:])
                    P.op('act', 'activation', R=[pT], W=[catT], out=catT[:, h, :], in_=pT[:], func=AF.Copy)
                pcs = [pX0, pX1]
                for gi in GROUPS:
                    wb = load_wb([(w_in16[:, 4096 + j * 1024 + gi * 128: 4096 + j * 1024 + (gi + 1) * 128], j * 128, 128) for j in range(3)])
                    for j in (0, 2, 1):
                        for si in range(2):
                            pc = pcs[si]
                            c0 = 512 * si
                            for c in range(16):
                                P.op('pe', 'matmul', R=[xT, wb], W=[pc], out=pc[:], lhsT=wb[:, c, j * 128:(j + 1) * 128],
                                     rhs=xT[:, c, c0:c0 + 512], start=(c == 0), stop=(c == 15))
                            if j == 0:
                                P.op('act', 'activation', R=[pc], W=[hcs], out=hcs[:, 1 + c0:1 + c0 + 512], in_=pc[:], func=AF.Copy)
                            elif j == 2:
                                P.op('dve', 'tensor_tensor', R=[pc, hcs], W=[us], out=us[:, 1 + c0:1 + c0 + 512], in0=pc[:], in1=hcs[:, 1 + c0:1 + c0 + 512], op=ALU.mult)
                            else:
                                P.op('act', 'activation', R=[pc], W=[bgs], out=bgs[:, c0:c0 + 512], in_=pc[:], func=AF.Copy)
                        if j != 1:
                            for c in range(16):
                                P.op('pe', 'matmul', R=[xH, wb], W=[pX2], out=pX2[:, 0:2], lhsT=wb[:, c, j * 128:(j + 1) * 128],
                                     rhs=xH[:, c, 0:2], start=(c == 0), stop=(c == 15))
                            if j == 0:
                                P.op('act', 'activation', R=[pX2], W=[hcs], out=hcs[:, 0:1], in_=pX2[:, 0:1], func=AF.Copy)
                                P.op('act', 'activation', R=[pX2], W=[hcs], out=hcs[:, 1025:1026], in_=pX2[:, 1:2], func=AF.Copy)
                            else:
                                P.op('dve', 'tensor_tensor', R=[pX2, hcs], W=[us], out=us[:, 0:1], in0=pX2[:, 0:1], in1=hcs[:, 0:1], op=ALU.mult)
                                P.op('dve', 'tensor_tensor', R=[pX2, hcs], W=[us], out=us[:, 1025:1026], in0=pX2[:, 1:2], in1=hcs[:, 1025:1026], op=ALU.mult)
                    P.op('dve', 'tensor_scalar', R=[us, cw], W=[cv], out=cv[:], in0=us[:, 0:1024], scalar1=cw[:, gi, 0:1], scalar2=None, op0=ALU.mult)
                    P.op('dve', 'scalar_tensor_tensor', R=[us, cw, cv], W=[cv], out=cv[:], in0=us[:, 1:1025], scalar=cw[:, gi, 1:2], in1=cv[:], op0=ALU.mult, op1=ALU.add)
                    P.op('dve', 'scalar_tensor_tensor', R=[us, cw, cv], W=[cv], out=cv[:], in0=us[:, 2:1026], scalar=cw[:, gi, 2:3], in1=cv[:], op0=ALU.mult, op1=ALU.add)
                    P.op('dve', 'tensor_tensor', R=[cv, bgs], W=[catT], out=catT[:, 8 + gi, :], in0=cv[:], in1=bgs[:], op=ALU.mult)
                dump('catT', catT, catT[:], [128, 16, 1024], BF16)
                P.barrier()

            with ExitStack() as s2:
                xt = SB("xt", [128, 2048], F32, s2)
                z = SB("z", [128, 2048], F32, s2)
                zq = SB("zq", [128, 2048], F32, s2)
                hh = SB("hh", [128, 2048], F32, s2)
                hb = SB("hb", [128, 2048], BF16, s2)
                hT = SB("hT", [128, 16, 128], BF16, s2)
                qp = SB("qp", [128, 2048], BF16, s2)
                qpT = hT
                vG = SB("vG", [128, 2048], F32, s2)
                vB = vG
                st2 = SB("st2", [128, 4], F32, s2)
                v16 = SB("v16", [128, 16, 16], F32, s2)
                ix16 = SB("ix16", [128, 16, 16], U32, s2)
                ixf = SB("ixf", [128, 16, 16], F32, s2)
                tmp = SB("tmp", [128, 256], F32, s2)
                best = SB("best", [128, 8, 16], F32, s2)
                pos = SB("pos", [128, 8, 16], U32, s2)
                pab = SB("pab", [128, 2, 8, 16], U32, s2)
                pabf = SB("pabf", [128, 2, 8, 16], F32, s2)
                isel = SB("isel", [128, 2, 8, 16], F32, s2)
                eidf = SB("eidf", [128, 128], F32, s2)
                gate = SB("gate", [128, 8, 16], F32, s2)
                gst = SB("gst", [128, 16], F32, s2)
                eidI = SB("eidI", [128, 128], I32, s2)
                actT = SB("actT", [128, 128], F32, s2)
                coefT = SB("coefT", [128, 128], F32, s2)
                idf = SB("idf", [128, 128], F32, s2)
                UVs = [SB("UV%d" % i, [128, 4096], BF16, s2) for i in range(NUV)]
                print("sbuf remaining (scope2)", nc.sbuf_bytes_remaining)
                actC = [Buf(actT.t) for _ in range(128)]
                coefC = [Buf(coefT.t) for _ in range(128)]
                dg = [SB("dg%d" % i, [128, 128], BF16, s2) for i in range(3)]
                P.op('dve', 'tensor_copy', R=[ct], W=[idf], out=idf[:], in_=ct[:, C_ID:C_ID + 128])
                cand = z
                oh = zq

                def layer_norm(src, dst, gi, extra_dst=None):
                    P.dma('sp', vG, vG[:], lnv[gi], W=[vG])
                    P.op('dve', 'reduce_sum', R=[src], W=[st2], out=st2[:, 0:1], in_=src[:], axis=AX.X)
                    P.op('dve', 'tensor_scalar', R=[st2], W=[st2], out=st2[:, 1:2], in0=st2[:, 0:1], scalar1=-1.0 / 2048, scalar2=None, op0=ALU.mult)
                    P.op('dve', 'tensor_scalar', R=[src, st2], W=[src], out=src[:], in0=src[:], scalar1=st2[:, 1:2], scalar2=None, op0=ALU.add)
                    P.op('pool', 'tensor_tensor', R=[src], W=[zq], out=zq[:], in0=src[:], in1=src[:], op=ALU.mult)
                    P.op('dve', 'reduce_sum', R=[zq], W=[st2], out=st2[:, 2:3], in_=zq[:], axis=AX.X)
                    P.op('dve', 'tensor_scalar', R=[st2], W=[st2], out=st2[:, 3:4], in0=st2[:, 2:3], scalar1=1.0 / 2048, scalar2=LN_EPS, op0=ALU.mult, op1=ALU.add)
                    P.op('act', 'activation', R=[st2], W=[st2], out=st2[:, 3:4], in_=st2[:, 3:4], func=AF.Sqrt)
                    P.op('dve', 'reciprocal', R=[st2], W=[st2], out=st2[:, 3:4], in_=st2[:, 3:4])
                    P.op('dve', 'scalar_tensor_tensor', R=[src, st2, vG], W=[dst], out=dst[:], in0=src[:], scalar=st2[:, 3:4], in1=vG[:], op0=ALU.mult, op1=ALU.mult)
                    P.dma('sp', vB, vB[:], lnv[gi + 1], W=[vB])
                    P.op('pool', 'tensor_tensor', R=[dst, vB], W=[dst], out=dst[:], in0=dst[:], in1=vB[:], op=ALU.add)

                if cat_in is not None:
                    P.dma('pool', catT, catT[:], cat_in, W=[catT])
                for n in TILES:
                    ts_ = slice(128 * n, 128 * (n + 1))
                    P.dma('sp', xt, xt[:], xtok[s, ts_, :], W=[xt])
                    for cb in range(4):
                        wb = load_wb([(w_out16[:, cb * 512:(cb + 1) * 512], 0, 512)])
                        for c in range(16):
                            P.op('pe', 'matmul', R=[catT, wb], W=[pM], out=pM[:, cb * 512:(cb + 1) * 512], lhsT=catT[:, c, ts_],
                                 rhs=wb[:, c, :], start=(c == 0), stop=(c == 15))
                    P.op('dve', 'scalar_tensor_tensor', R=[xt, pM], W=[z], out=z[:], in0=xt[:], scalar=ALPHA, in1=pM[:], op0=ALU.mult, op1=ALU.add)
                    layer_norm(z, hh, 0)
                    dump('h1', hh, hh[:], [128, 2048])
                    P.op('act', 'activation', R=[hh], W=[hb], out=hb[:], in_=hh[:], func=AF.Copy)
                    for c in range(16):
                        P.op('pe', 'transpose', R=[hb, idb], W=[pT], out=pT[:, (c % 8) * 128:(c % 8 + 1) * 128], in_=hb[:, c * 128:(c + 1) * 128], identity=idb[:])
                        if c % 8 == 7:
                            c0 = c - 7
                            P.op('act', 'activation', R=[pT], W=[hT], out=hT[:, c0:c0 + 8, :].rearrange("p c t -> p (c t)"), in_=pT[:], func=AF.Copy)
                    for cb in range(4):
                        wb = load_wb([(w_q16[:, cb * 512:(cb + 1) * 512], 0, 512)])
                        for c in range(16):
                            P.op('pe', 'matmul', R=[hT, wb], W=[pX0], out=pX0[:], lhsT=hT[:, c, :], rhs=wb[:, c, :], start=(c == 0), stop=(c == 15))
                        P.op('act', 'activation', R=[pX0], W=[qp], out=qp[:, cb * 512:(cb + 1) * 512], in_=pX0[:], func=AF.Copy)
                    for c in range(16):
                        P.op('pe', 'transpose', R=[qp, idb], W=[pT], out=pT[:, (c % 8) * 128:(c % 8 + 1) * 128], in_=qp[:, c * 128:(c + 1) * 128], identity=idb[:])
                        if c % 8 == 7:
                            c0 = c - 7
                            P.op('act', 'activation', R=[pT], W=[qpT], out=qpT[:, c0:c0 + 8, :].rearrange("p c t -> p (c t)"), in_=pT[:], func=AF.Copy)
                    for mm in range(16):
                        P.op('pe', 'matmul', R=[qpT, kT16], W=[pM], out=pM[:, mm * 128:(mm + 1) * 128], lhsT=qpT[:, mm, :], rhs=kT16[:, mm, :], start=True, stop=True)
                    sc = xt
                    P.op('act', 'activation', R=[pM], W=[sc], out=sc[:], in_=pM[:], func=AF.Copy)
                    dump('sc', sc, sc[:], [128, 2048])
                    for mm in range(16):
                        srow = sc[:, mm * 128:(mm + 1) * 128]
                        P.op('dve', 'max', R=[sc], W=[v16], out=v16[:, mm, 0:8], in_=srow)
                        P.op('dve', 'max_index', R=[sc, v16], W=[ix16], out=ix16[:, mm, 0:8], in_max=v16[:, mm, 0:8], in_values=srow)
                        P.op('dve', 'match_replace', R=[sc, v16], W=[tmp], out=tmp[:, 0:128], in_to_replace=v16[:, mm, 0:8], in_values=srow, imm_value=-1e30)
                        P.op('dve', 'max', R=[tmp], W=[v16], out=v16[:, mm, 8:16], in_=tmp[:, 0:128])
                        P.op('dve', 'max_index', R=[tmp, v16], W=[ix16], out=ix16[:, mm, 8:16], in_max=v16[:, mm, 8:16], in_values=tmp[:, 0:128])
                    P.op('dve', 'tensor_copy', R=[ix16], W=[ixf], out=ixf[:], in_=ix16[:])
                    v4 = v16[:].rearrange("p (h two) k -> p h two k", two=2)
                    i4 = ixf[:].rearrange("p (h two) k -> p h two k", two=2)
                    c4 = cand[:].rearrange("p (h a b) -> p h a b", h=8, a=16)
                    P.op('dve', 'tensor_tensor', R=[v16], W=[cand], out=c4, in0=v4[:, :, 0, :].unsqueeze(3).to_broadcast([128, 8, 16, 16]),
                         in1=v4[:, :, 1, :].unsqueeze(2).to_broadcast([128, 8, 16, 16]), op=ALU.add)
                    for h in range(8):
                        crow = cand[:, h * 256:(h + 1) * 256]
                        P.op('dve', 'max', R=[cand], W=[best], out=best[:, h, 0:8], in_=crow)
                        P.op('dve', 'max_index', R=[cand, best], W=[pos], out=pos[:, h, 0:8], in_max=best[:, h, 0:8], in_values=crow)
                        P.op('dve', 'match_replace', R=[cand, best], W=[tmp], out=tmp[:], in_to_replace=best[:, h, 0:8], in_values=crow, imm_value=-1e30)
                        P.op('dve', 'max', R=[tmp], W=[best], out=best[:, h, 8:16], in_=tmp[:])
                        P.op('dve', 'max_index', R=[tmp, best], W=[pos], out=pos[:, h, 8:16], in_max=best[:, h, 8:16], in_values=tmp[:])
                    P.op('dve', 'tensor_single_scalar', R=[pos], W=[pab], out=pab[:, 0], in_=pos[:], scalar=4, op=ALU.logical_shift_right)
                    P.op('dve', 'tensor_single_scalar', R=[pos], W=[pab], out=pab[:, 1], in_=pos[:], scalar=15, op=ALU.bitwise_and)
                    P.op('dve', 'tensor_copy', R=[pab], W=[pabf], out=pabf[:], in_=pab[:])
                    o4 = oh[:].rearrange("p (h k a) -> p h k a", h=8, k=16)
                    iob = ct[:, C_IOTA:C_IOTA + 16].unsqueeze(1).unsqueeze(1).to_broadcast([128, 8, 16, 16])
                    for w_ in range(2):
                        P.op('dve', 'tensor_tensor', R=[pabf, ct], W=[oh], out=o4, in0=pabf[:, w_].unsqueeze(3).to_broadcast([128, 8, 16, 16]), in1=iob, op=ALU.is_equal)
                        P.op('dve', 'tensor_tensor', R=[oh, ixf], W=[oh], out=o4, in0=o4, in1=i4[:, :, w_, :].unsqueeze(2).to_broadcast([128, 8, 16, 16]), op=ALU.mult)
                        P.op('dve', 'tensor_reduce', R=[oh], W=[isel], out=isel[:, w_], in_=o4, axis=AX.X, op=ALU.add)
                    P.op('dve', 'scalar_tensor_tensor', R=[isel], W=[eidf], out=eidf[:].rearrange("p (h k) -> p h k", h=8), in0=isel[:, 0], scalar=128.0, in1=isel[:, 1], op0=ALU.mult, op1=ALU.add)
                    P.op('dve', 'tensor_tensor', R=[best], W=[gate], out=gate[:], in0=best[:], in1=best[:, :, 0:1].to_broadcast([128, 8, 16]), op=ALU.subtract)
                    P.op('act', 'activation', R=[gate], W=[gate], out=gate[:], in_=gate[:], func=AF.Exp)
                    P.op('dve', 'tensor_reduce', R=[gate], W=[gst], out=gst[:, 0:8], in_=gate[:], axis=AX.X, op=ALU.add)
                    P.op('dve', 'reciprocal', R=[gst], W=[gst], out=gst[:, 8:16], in_=gst[:, 0:8])
                    P.op('dve', 'tensor_tensor', R=[gate, gst], W=[gate], out=gate[:], in0=gate[:], in1=gst[:, 8:16].unsqueeze(2).to_broadcast([128, 8, 16]), op=ALU.mult)
                    dump('eidf', eidf, eidf[:], [128, 128])
                    dump('gate', gate, gate[:], [128, 8, 16])
                    P.op('dve', 'tensor_copy', R=[eidf], W=[eidI], out=eidI[:], in_=eidf[:])
                    P.op('pool', 'memset', W=actC, ap=actT[:], constant=0.0)
                    gate2 = gate[:].rearrange("p h k -> p (h k)")

                    def gUV(k):
                        ub = UVs[k % NUV]
                        P.dma('pool', ub, ub[:], uv16, R=[eidI, UV16], W=[ub], name='indirect_dma_start', out_offset=None,
                              in_offset=bass.IndirectOffsetOnAxis(ap=eidI[:, k:k + 1], axis=0))

                    def vstep(k):
                        vb_ = UVs[k % NUV]
                        dgb = dg[k % len(dg)]
                        P.op('act', 'activation', R=[coefC[k], gate], W=[coefC[k]], out=coefT[:, k:k + 1], in_=coefT[:, k:k + 1], func=AF.Copy, scale=gate2[:, k:k + 1])
                        P.op('act', 'activation', R=[idf, coefC[k]], W=[dgb], out=dgb[:], in_=idf[:], func=AF.Copy, scale=coefT[:, k:k + 1])
                        for q4 in range(4):
                            P.op('pe', 'matmul', R=[dgb, vb_], W=[pM], out=pM[:, q4 * 512:(q4 + 1) * 512], lhsT=dgb[:], rhs=vb_[:, 2048 + q4 * 512:2048 + (q4 + 1) * 512],
                                 start=(k == 0), stop=(k == NTOK - 1))

                    PF = NUV - 2
                    for k in range(min(PF, NTOK)):
                        gUV(k)
                    for k in range(NTOK):
                        if k + PF < NTOK:
                            gUV(k + PF)
                        ub = UVs[k % NUV]
                        P.op('dve', 'scalar_tensor_tensor', R=[ub, hb], W=[qp, actC[k]], out=qp[:], in0=ub[:, 0:2048], scalar=1.0, in1=hb[:], op0=ALU.mult, op1=ALU.mult,
                             accum_out=actT[:, k:k + 1])
                        P.op('act', 'activation', R=[actC[k]], W=[coefC[k]], out=coefT[:, k:k + 1], in_=actT[:, k:k + 1], func=AF.Gelu)
                        if k >= 1:
                            vstep(k - 1)
                    vstep(NTOK - 1)
                    if 'actT' in dbg.get('dump', []):
                        P.op('dve', 'tensor_copy', R=actC, W=[actT], out=actT[:], in_=actT[:])
                    dump('actT', actT, actT[:], [128, 128])
                    P.op('dve', 'scalar_tensor_tensor', R=[hh, pM], W=[z], out=z[:], in0=hh[:], scalar=ALPHA, in1=pM[:], op0=ALU.mult, op1=ALU.add)
                    dump('z2', z, z[:], [128, 2048])
                    layer_norm(z, hh, 2)
                    P.dma('pool', hh, y[s, ts_, :], hh[:], R=[hh])
                P.barrier()
        P.barrier()

        with nc.Block() as block:
            @block.tensor
            def _(E):
                P.replay('pe', E)

            @block.vector
            def _(E):
                P.replay('dve', E)

            @block.scalar
            def _(E):
                P.replay('act', E)

            @block.gpsimd
            def _(E):
                P.replay('pool', E)

            @block.sync
            def _(E):
                P.replay('sp', E)
    return nc


def _host_tables(c):
    j = np.arange(128, dtype=np.float64)[:, None]
    i = np.arange(128, dtype=np.float64)[None, :]
    t = np.zeros((128, NT), np.float32)
    t[:, C_EPF:C_EPF + 128] = np.maximum(i - j, 0)
    t[:, C_MPF:C_MPF + 128] = (i >= j)
    t[:, C_EPB:C_EPB + 128] = np.maximum(j - i, 0)
    t[:, C_MPB:C_MPB + 128] = (j >= i)
    t[:, C_EXF:C_EXF + 128] = i + 1
    t[:, C_EXB:C_EXB + 128] = 128 - i
    t[:, C_EZF1] = 127 - j[:, 0]
    t[:, C_EZB1] = j[:, 0]
    m = np.arange(8)[None, :]
    t[:, C_EZFA:C_EZFA + 8] = 1023 - (128 * m + j)
    t[:, C_EZBA:C_EZBA + 8] = 128 * m + j
    rl = np.arange(8)
    r = rl + (rl >= c)
    valid = (rl < 7)
    t[:, C_EWF:C_EWF + 8] = (1024 * np.maximum(c - 1 - r, 0) * valid)[None, :]
    t[:, C_MWF:C_MWF + 8] = ((r < c) & valid)[None, :]
    t[:, C_EWB:C_EWB + 8] = (1024 * np.maximum(r - c - 1, 0) * valid)[None, :]
    t[:, C_MWB:C_MWB + 8] = ((r > c) & valid)[None, :]
    t[:, C_IOTA:C_IOTA + 16] = np.arange(16)[None, :]
    t[:, C_ID:C_ID + 128] = np.eye(128)
    return t


_NC_CACHE = {}


def _prep(x_prompt, x_sample, w_in, ret_log_rate_fwd, ret_log_rate_bwd, ret_gn_g, conv_w, w_out,
          ln1_g, ln1_b, peer_w_query, peer_keys_1, peer_keys_2, peer_u, peer_v, ln2_g, ln2_b, cores=range(8)):
    f = np.float32
    xs = [np.asarray(x_prompt[0], f), np.asarray(x_sample[0], f), np.asarray(x_sample[1], f)]
    xTfull = [np.ascontiguousarray(x.T) for x in xs]
    inv = (np.float32(10000.0) ** (-np.arange(0, 128, 2, dtype=np.float32) / np.float32(128))).astype(np.float32)
    ang = (np.arange(8192, dtype=np.float32)[:, None] * inv[None, :]).astype(np.float32)
    cosT = np.cos(ang.astype(np.float64)).astype(f)
    sinT = np.sin(ang.astype(np.float64)).astype(f)
    cs = np.stack([cosT, sinT])
    csall = np.ascontiguousarray(cs.reshape(2, 64, 128, 64).transpose(2, 0, 1, 3))
    keys = np.stack([np.asarray(peer_keys_1[0], f), np.asarray(peer_keys_2[0], f)], axis=1)
    keysT = np.ascontiguousarray(keys.reshape(16, 128, 128).transpose(2, 0, 1))
    rep = lambda v: np.ascontiguousarray(np.broadcast_to(np.asarray(v, f).reshape(1, -1), (128, np.asarray(v).size)))
    lnv = np.stack([rep(ln1_g[0]), rep(ln1_b[0]), rep(ln2_g[0]), rep(ln2_b[0])])
    gng = rep(ret_gn_g[0])
    cwT = np.ascontiguousarray(np.asarray(conv_w[0], f).T.reshape(8, 128, 3).transpose(1, 0, 2))
    rates = np.concatenate([np.asarray(ret_log_rate_fwd[0], f), np.asarray(ret_log_rate_bwd[0], f)])
    w_in0 = np.ascontiguousarray(np.asarray(w_in[0], f))
    w_out0 = np.ascontiguousarray(np.asarray(w_out[0], f))
    w_q0 = np.ascontiguousarray(np.asarray(peer_w_query[0], f))
    pu0 = np.ascontiguousarray(np.asarray(peer_u[0], f))
    pv0 = np.ascontiguousarray(np.asarray(peer_v[0], f))
    in_maps = []
    for c in cores:
        lo = 1024 * c
        xtok = np.ascontiguousarray(np.stack([x[lo:lo + 1024] for x in xs]))
        xTs = np.zeros((3, 2048, 1024), f)
        xTh = np.zeros((3, 2048, 64), f)
        for s in range(3):
            xTs[s] = xs[s][lo:lo + 1024].T
            if lo - 1 >= 0:
                xTh[s, :, 0] = xs[s][lo - 1]
            if lo + 1024 < 8192:
                xTh[s, :, 1] = xs[s][lo + 1024]
        xTall = np.ascontiguousarray(np.stack([np.concatenate([xt_[:, :lo], xt_[:, lo + 1024:]], axis=1) for xt_ in xTfull]))
        oth = [g_ for g_ in range(64) if not (8 * c <= g_ < 8 * c + 8)]
        csoth = np.ascontiguousarray(csall[:, :, oth, :])
        ct = _host_tables(c)
        ct[:, C_RATE:C_RATE + 16] = rates[None, :]
        csown = np.ascontiguousarray(csall[:, :, 8 * c:8 * c + 8, :])
        in_maps.append(dict(xtok=xtok, xTs=xTs, xTh=xTh, xTall=xTall, w_in=w_in0, w_out=w_out0, w_q=w_q0, keysT=keysT,
                            pu=pu0, pv=pv0, ctab=ct, csall=csoth, csown=csown, gng=gng, cwT=cwT, lnv=lnv))
    return in_maps


def kernel(**inputs):
    f = np.float32
    in_maps = _prep(**inputs)
    if 'nc' not in _NC_CACHE:
        _NC_CACHE['nc'] = build()
    res = run_bass_kernel_spmd(_NC_CACHE['nc'], in_maps, core_ids=list(range(8)))
    ys = [np.zeros((8192, 2048), f) for _ in range(3)]
    for c in range(8):
        yc = res.results[c]["y"]
        for s in range(3):
            ys[s][1024 * c:1024 * (c + 1)] = yc[s]
    return (ys[0][None], np.stack([ys[1], ys[2]]))
```

```python
import numpy as np
from contextlib import ExitStack
import concourse.bass as bass
import concourse.mybir as mybir
from concourse.bass_utils import run_bass_kernel_spmd

F32 = mybir.dt.float32
BF16 = mybir.dt.bfloat16
I32 = mybir.dt.int32
U32 = mybir.dt.uint32
AF = mybir.ActivationFunctionType
ALU = mybir.AluOpType
AX = mybir.AxisListType

ALPHA = 2.0 ** 0.25
LN_EPS = 1e-5
GN_EPS = 1e-6
SK = 128.0 ** -0.5
NSEG = 3
NBLK = 7
NCH = 8

C_RATE = 0
C_EPF, C_MPF, C_EPB, C_MPB = 16, 144, 272, 400
C_EXF, C_EXB = 528, 656
C_EZF1, C_EZB1 = 784, 785
C_EZFA, C_EZBA = 786, 794
C_EWF, C_MWF, C_EWB, C_MWB = 802, 810, 818, 826
C_IOTA = 834
C_ID = 850
NT = 978


class Buf:
    def __init__(self, t):
        self.t = t
        self.w = {}
        self.r = {}
        self.dkey = None
        self.dcnt = 0
        self.psum = False

    def __getitem__(self, k):
        return self.t[k]


class Prog:
    def __init__(self, nc, es):
        self.nc = nc
        self.es = es
        self.names = ['pe', 'dve', 'act', 'pool', 'sp']
        self.semh = {}
        for k in self.names:
            self.semh[k] = es.enter_context(nc.semaphore('s_' + k))
        self.cnt = {k: 0 for k in self.names}
        self.waited = {k: {} for k in self.names}
        self.q = {k: [] for k in self.names}
        self.nd = 0
        self.dbufs = []

    def _need(self, reads, writes, e=None):
        need = {}
        for b in reads:
            for k, v in b.w.items():
                if need.get(k, 0) < v:
                    need[k] = v
            if b.psum:
                for k, v in b.r.items():
                    if k != e and need.get(k, 0) < v:
                        need[k] = v
        for b in writes:
            for k, v in b.w.items():
                if need.get(k, 0) < v:
                    need[k] = v
            for k, v in b.r.items():
                if need.get(k, 0) < v:
                    need[k] = v
        return need

    def _waits(self, e, need):
        waits = []
        wd = self.waited[e]
        for k, v in need.items():
            if k == e and e == 'pe':
                continue
            if wd.get(k, 0) < v:
                wd[k] = v
                waits.append((self.semh[k], v))
        return waits

    def op(self, e, name, R=(), W=(), **kw):
        waits = self._waits(e, self._need(R, W, e))
        self.cnt[e] += 1
        tok = self.cnt[e]
        self.q[e].append((waits, name, kw, self.semh[e], 1))
        for b in R:
            if b.r.get(e, 0) < tok:
                b.r[e] = tok
        for b in W:
            b.w[e] = tok

    def dma(self, e, sb, out, in_, R=(), W=(), name='dma_start', nodep=False, **kw):
        if sb.dkey is None:
            sb.dkey = 'd%d' % self.nd
            self.nd += 1
            self.semh[sb.dkey] = self.es.enter_context(self.nc.semaphore(sb.dkey))
            self.dbufs.append(sb)
        waits = [] if nodep else self._waits(e, self._need(R, W))
        sb.dcnt += 16
        kw = dict(kw)
        kw['out'] = out
        kw['in_'] = in_
        self.q[e].append((waits, name, kw, self.semh[sb.dkey], 16))
        for b in R:
            if b.r.get(sb.dkey, 0) < sb.dcnt:
                b.r[sb.dkey] = sb.dcnt
        for b in W:
            b.w[sb.dkey] = sb.dcnt

    def barrier(self):
        need = {k: self.cnt[k] for k in self.names if self.cnt[k] > 0}
        for b in self.dbufs:
            need[b.dkey] = b.dcnt
        for e in self.names:
            waits = self._waits(e, need)
            if waits:
                self.q[e].append((waits, None, None, None, 0))

    def replay(self, e, E):
        for waits, name, kw, sem, inc in self.q[e]:
            for s, v in waits:
                E.wait_ge(s, v)
            if name is None:
                continue
            ins = getattr(E, name)(**kw)
            ins.then_inc(sem, inc)


def build(dbg=None):
    nc = bass.Bass("TRN2", target_bir_lowering=False)
    dbg = dbg or {}
    A_BLOCKS = dbg.get('a_blocks', [(s_, r_) for s_ in range(NSEG) for r_ in range(NBLK)])
    SEGS = dbg.get('segs', list(range(NSEG)))
    HEADS = dbg.get('heads', list(range(8)))
    GROUPS = dbg.get('groups', list(range(8)))
    TILES = dbg.get('tiles', list(range(NCH)))
    NTOK = dbg.get('ntok', 128)
    HST = dbg.get('hstage', 99)
    NUV = 6
    dumps = []

    def din(name, shape, dt=F32):
        return nc.dram_tensor(name, shape, dt, kind="ExternalInput").ap()

    xtok = din("xtok", [NSEG, 1024, 2048])
    xTs = din("xTs", [NSEG, 2048, 1024])
    xTh = din("xTh", [NSEG, 2048, 64])
    xTall = din("xTall", [NSEG, 2048, 1024 * NBLK])
    w_in = din("w_in", [2048, 7168])
    w_out = din("w_out", [2048, 2048])
    w_q = din("w_q", [2048, 2048])
    keysT = din("keysT", [128, 16, 128])
    pu = din("pu", [16384, 2048])
    pv = din("pv", [16384, 2048])
    ctab = din("ctab", [128, NT])
    csall = din("csall", [128, 2, 8 * NBLK, 64])
    csown = din("csown", [128, 2, 8, 64])
    gng = din("gng", [128, 1024])
    cwT = din("cwT", [128, 8, 3])
    lnv = din("lnv", [4, 128, 2048])
    y = nc.dram_tensor("y", [NSEG, 1024, 2048], F32, kind="ExternalOutput").ap()
    sin_d = nc.dram_tensor("sin_d", [NSEG, 2, 128, 1024], F32, kind=("ExternalInput" if dbg.get('sin_in') else "Internal")).ap()
    cat_in = din("cat_in", [128, 16, 1024]) if dbg.get('cat_in') else None
    uv16 = nc.dram_tensor("uv16", [16384, 4096], BF16, kind="Internal").ap()
    w_in16 = nc.dram_tensor("w_in16", [2048, 7168], BF16, kind="Internal").ap()
    w_out16 = nc.dram_tensor("w_out16", [2048, 2048], BF16, kind="Internal").ap()
    w_q16 = nc.dram_tensor("w_q16", [2048, 2048], BF16, kind="Internal").ap()

    es = ExitStack()
    with es:
        P = Prog(nc, es)

        uid = [0]

        def SB(name, shape, dt=F32, stack=es):
            uid[0] += 1
            return Buf(stack.enter_context(nc.sbuf_tensor("%s_%d" % (name, uid[0]), shape, dt)))

        def PS(name, shape, dt=F32, stack=es):
            uid[0] += 1
            b = Buf(stack.enter_context(nc.psum_tensor("%s_%d" % (name, uid[0]), shape, dt)))
            b.psum = True
            return b

        def dump(name, buf, ap, shape, dt=F32):
            if name not in dbg.get('dump', []):
                return
            d = nc.dram_tensor("dbg_" + name, shape, dt, kind="ExternalOutput").ap()
            P.dma('sp', buf, d, ap, R=[buf])

        ct = SB("ct", [128, NT])
        idb = SB("idb", [128, 128], BF16)
        lg = SB("lg", [128, 16])
        DmT = SB("DmT", [128, 8, 128])
        XiF = SB("XiF", [128, 8, 128])
        XiB = SB("XiB", [128, 8, 128])
        Z1 = SB("Z1", [128, 16])
        ZA = SB("ZA", [128, 2, 8, 8])
        G128 = SB("G128", [128, 16])
        Wc = SB("Wc", [128, 2, 8, 8])
        cso = SB("cso", [128, 2, 8, 64])
        gn = SB("gn", [128, 1024])
        cw = SB("cw", [128, 8, 3])
        kT16 = SB("kT16", [128, 16, 128], BF16)
        tA = SB("tA", [128, 128])
        tB = SB("tB", [128, 128])

        P.dma('sp', ct, ct[:], ctab, W=[ct])
        P.dma('pool', idb, idb[:], ctab[:, C_ID:C_ID + 128], W=[idb])
        P.dma('sp', cso, cso[:], csown, W=[cso])
        P.dma('sp', gn, gn[:], gng, W=[gn])
        P.dma('sp', cw, cw[:], cwT, W=[cw])
        P.dma('pool', kT16, kT16[:], keysT, W=[kT16])

        P.op('act', 'activation', R=[ct], W=[lg], out=lg[:], in_=ct[:, C_RATE:C_RATE + 16], func=AF.Exp)
        P.op('dve', 'tensor_scalar', R=[lg], W=[lg], out=lg[:], in0=lg[:], scalar1=-1.0, scalar2=None, op0=ALU.mult)
        for h in range(8):
            lf = lg[:, h:h + 1]
            lb = lg[:, 8 + h:9 + h]
            P.op('act', 'activation', R=[ct, lg], W=[tA], out=tA[:], in_=ct[:, C_EPF:C_EPF + 128], func=AF.Exp, scale=lf)
            P.op('dve', 'tensor_tensor', R=[tA, ct], W=[tA], out=tA[:], in0=tA[:], in1=ct[:, C_MPF:C_MPF + 128], op=ALU.mult)
            P.op('act', 'activation', R=[ct, lg], W=[tB], out=tB[:], in_=ct[:, C_EPB:C_EPB + 128], func=AF.Exp, scale=lb)
            P.op('dve', 'tensor_tensor', R=[tB, ct], W=[tB], out=tB[:], in0=tB[:], in1=ct[:, C_MPB:C_MPB + 128], op=ALU.mult)
            P.op('dve', 'tensor_tensor', R=[tA, tB], W=[DmT], out=DmT[:, h, :], in0=tA[:], in1=tB[:], op=ALU.add)
            P.op('act', 'activation', R=[ct, lg], W=[XiF], out=XiF[:, h, :], in_=ct[:, C_EXF:C_EXF + 128], func=AF.Exp, scale=lf)
            P.op('act', 'activation', R=[ct, lg], W=[XiB], out=XiB[:, h, :], in_=ct[:, C_EXB:C_EXB + 128], func=AF.Exp, scale=lb)
            P.op('act', 'activation', R=[ct, lg], W=[ZA], out=ZA[:, 0, :, h], in_=ct[:, C_EZFA:C_EZFA + 8], func=AF.Exp, scale=lf)
            P.op('act', 'activation', R=[ct, lg], W=[ZA], out=ZA[:, 1, :, h], in_=ct[:, C_EZBA:C_EZBA + 8], func=AF.Exp, scale=lb)
            P.op('act', 'activation', R=[ct, lg], W=[Wc], out=Wc[:, 0, :, h], in_=ct[:, C_EWF:C_EWF + 8], func=AF.Exp, scale=lf)
            P.op('act', 'activation', R=[ct, lg], W=[Wc], out=Wc[:, 1, :, h], in_=ct[:, C_EWB:C_EWB + 8], func=AF.Exp, scale=lb)
        P.op('dve', 'tensor_scalar', R=[DmT], W=[DmT], out=DmT[:], in0=DmT[:], scalar1=SK, scalar2=None, op0=ALU.mult)
        P.op('dve', 'tensor_scalar', R=[ZA], W=[ZA], out=ZA[:], in0=ZA[:], scalar1=SK, scalar2=None, op0=ALU.mult)
        P.op('dve', 'tensor_tensor', R=[Wc, ct], W=[Wc], out=Wc[:, 0], in0=Wc[:, 0],
             in1=ct[:, C_MWF:C_MWF + 8].unsqueeze(2).to_broadcast([128, 8, 8]), op=ALU.mult)
        P.op('dve', 'tensor_tensor', R=[Wc, ct], W=[Wc], out=Wc[:, 1], in0=Wc[:, 1],
             in1=ct[:, C_MWB:C_MWB + 8].unsqueeze(2).to_broadcast([128, 8, 8]), op=ALU.mult)
        P.op('dve', 'tensor_scalar', R=[lg, ct], W=[Z1], out=Z1[:, 0:8], in0=lg[:, 0:8], scalar1=ct[:, C_EZF1:C_EZF1 + 1], scalar2=None, op0=ALU.mult)
        P.op('dve', 'tensor_scalar', R=[lg, ct], W=[Z1], out=Z1[:, 8:16], in0=lg[:, 8:16], scalar1=ct[:, C_EZB1:C_EZB1 + 1], scalar2=None, op0=ALU.mult)
        P.op('act', 'activation', R=[Z1], W=[Z1], out=Z1[:], in_=Z1[:], func=AF.Exp)
        P.op('dve', 'tensor_scalar', R=[Z1], W=[Z1], out=Z1[:], in0=Z1[:], scalar1=SK, scalar2=None, op0=ALU.mult)
        P.op('act', 'activation', R=[lg], W=[G128], out=G128[:], in_=lg[:], func=AF.Exp, scale=128.0)

        dump('DmT', DmT, DmT[:], [128, 8, 128])
        dump('XiF', XiF, XiF[:], [128, 8, 128])
        dump('XiB', XiB, XiB[:], [128, 8, 128])
        dump('Z1', Z1, Z1[:], [128, 16])
        dump('ZA', ZA, ZA[:], [128, 2, 8, 8])
        dump('Wc', Wc, Wc[:], [128, 2, 8, 8])
        dump('G128', G128, G128[:], [128, 16])

        UV16 = Buf(uv16)
        CONV_ROWS = 256
        W16 = Buf(w_in16)
        conv_jobs = [(W16, w_in16, w_in, i, 128) for i in range(16)] + [(W16, w_out16, w_out, i, 256) for i in range(8)] + \
                    [(W16, w_q16, w_q, i, 256) for i in range(8)] + \
                    [(UV16, uv16[:, 0:2048], pu, i, CONV_ROWS) for i in range(16384 // CONV_ROWS)] + \
                    [(UV16, uv16[:, 2048:4096], pv, i, CONV_ROWS) for i in range(16384 // CONV_ROWS)]
        conv_jobs.reverse()

        def conv_step(nmax=1):
            for _ in range(nmax):
                if not conv_jobs:
                    return
                B_, dst_, src_, i_, nr = conv_jobs.pop()
                P.dma('pool', B_, dst_[i_ * nr:(i_ + 1) * nr, :], src_[i_ * nr:(i_ + 1) * nr, :], W=[B_], nodep=True)

        def rotary(src, dst, cos, sin, H, t1, t2, t3, t4, srcB, dstB, csB):
            cb = cos.unsqueeze(1).to_broadcast([128, H, 64])
            sb_ = sin.unsqueeze(1).to_broadcast([128, H, 64])
            x1 = src[:, :, 0:64]
            x2 = src[:, :, 64:128]
            P.op('dve', 'tensor_tensor', R=[srcB, csB], W=[t1], out=t1[:, 0:H, :], in0=x1, in1=cb, op=ALU.mult)
            P.op('dve', 'tensor_tensor', R=[srcB, csB], W=[t2], out=t2[:, 0:H, :], in0=x2, in1=sb_, op=ALU.mult)
            P.op('dve', 'tensor_tensor', R=[t1, t2], W=[dstB], out=dst[:, :, 0:64], in0=t1[:, 0:H, :], in1=t2[:, 0:H, :], op=ALU.subtract)
            P.op('pool', 'tensor_tensor', R=[srcB, csB], W=[t3], out=t3[:, 0:H, :], in0=x1, in1=sb_, op=ALU.mult)
            P.op('pool', 'tensor_tensor', R=[srcB, csB], W=[t4], out=t4[:, 0:H, :], in0=x2, in1=cb, op=ALU.mult)
            P.op('pool', 'tensor_tensor', R=[t3, t4], W=[dstB], out=dst[:, :, 64:128], in0=t3[:, 0:H, :], in1=t4[:, 0:H, :], op=ALU.add)


        with ExitStack() as sa:
            Wkv = SB("Wkv", [128, 16, 2048], BF16, sa)
            rt = [SB("rtA%d" % i, [128, 8, 64], F32, sa) for i in range(4)]
            csA = SB("csA", [128, 2, 8 * NBLK, 64], F32, sa)
            xa = [SB("xa%d" % i, [128, 16, 128], BF16, sa) for i in range(2)]
            ksb = SB("ksb", [128, 8, 128], F32, sa)
            kr = SB("kr", [128, 8, 128], F32, sa)
            kfA = SB("kfA", [128, 8, 128], BF16, sa)
            kbA = SB("kbA", [128, 8, 128], BF16, sa)
            vA = SB("vA", [128, 1024], BF16, sa)
            SinAcc = [[SB("sin%d_%d" % (s, d), [128, 1024], F32, sa) for d in range(2)] for s in range(NSEG)]
            pk = PS("pk", [128, 1024], F32, sa)
            pvv = PS("pvv", [128, 1024], F32, sa)
            pAf = PS("pAf", [128, 1024], F32, sa)
            pAb = PS("pAb", [128, 1024], F32, sa)
            w_kv = w_in[:, 1024:3072].rearrange("(c p) n -> p c n", p=128)
            for c4 in range(4):
                P.dma('pool', Wkv, Wkv[:, 4 * c4:4 * c4 + 4, :], w_kv[:, 4 * c4:4 * c4 + 4, :], W=[Wkv])
            P.dma('sp', csA, csA[:], csall, W=[csA])
            for s in range(NSEG):
                for d in range(2):
                    P.op('pool', 'memset', W=[SinAcc[s][d]], ap=SinAcc[s][d][:], constant=0.0)
            it = 0
            for s in range(NSEG):
                for r in range(NBLK):
                    if (s, r) not in A_BLOCKS:
                        continue
                    for m in range(NCH):
                        g = 8 * r + m
                        xb_ = xa[it % 2]
                        it += 1
                        P.dma('pool', xb_, xb_[:], xTall[s, :, g * 128:(g + 1) * 128].rearrange("(c p) t -> p c t", p=128), W=[xb_])
                        conv_step(1)
                        for half in range(2):
                            for c in range(16):
                                P.op('pe', 'matmul', R=[xb_, Wkv], W=[pk], out=pk[:, half * 512:(half + 1) * 512], lhsT=xb_[:, c, :],
                                     rhs=Wkv[:, c, half * 512:(half + 1) * 512], start=(c == 0), stop=(c == 15))
                        for half in range(2):
                            for c in range(16):
                                P.op('pe', 'matmul', R=[xb_, Wkv], W=[pvv], out=pvv[:, half * 512:(half + 1) * 512], lhsT=xb_[:, c, :],
                                     rhs=Wkv[:, c, 1024 + half * 512:1024 + (half + 1) * 512], start=(c == 0), stop=(c == 15))
                        P.op('act', 'activation', R=[pk], W=[ksb], out=ksb[:].rearrange("p h d -> p (h d)"), in_=pk[:], func=AF.Copy)
                        P.op('act', 'activation', R=[pvv], W=[vA], out=vA[:], in_=pvv[:], func=AF.Copy)
                        rotary(ksb[:], kr[:], csA[:, 0, g, :], csA[:, 1, g, :], 8, rt[0], rt[1], rt[2], rt[3], ksb, kr, csA)
                        P.op('dve', 'tensor_tensor', R=[kr, ZA], W=[kfA], out=kfA[:], in0=kr[:],
                             in1=ZA[:, 0, m, :].unsqueeze(2).to_broadcast([128, 8, 128]), op=ALU.mult)
                        P.op('pool', 'tensor_tensor', R=[kr, ZA], W=[kbA], out=kbA[:], in0=kr[:],
                             in1=ZA[:, 1, m, :].unsqueeze(2).to_broadcast([128, 8, 128]), op=ALU.mult)
                        for h in range(8):
                            P.op('pe', 'matmul', R=[kfA, vA], W=[pAf], out=pAf[:, h * 128:(h + 1) * 128], lhsT=kfA[:, h, :],
                                 rhs=vA[:, h * 128:(h + 1) * 128], start=(m == 0 and h % 4 == 0), stop=(m == 7), skip_group_check=True)
                            P.op('pe', 'matmul', R=[kbA, vA], W=[pAb], out=pAb[:, h * 128:(h + 1) * 128], lhsT=kbA[:, h, :],
                                 rhs=vA[:, h * 128:(h + 1) * 128], start=(m == 0 and h % 4 == 0), stop=(m == 7), skip_group_check=True)
                    for d, pA in ((0, pAf), (1, pAb)):
                        acc = SinAcc[s][d]
                        for h in range(8):
                            P.op('dve', 'scalar_tensor_tensor', R=[pA, Wc, acc], W=[acc], out=acc[:, h * 128:(h + 1) * 128],
                                 in0=pA[:, h * 128:(h + 1) * 128], scalar=Wc[:, d, r, h:h + 1], in1=acc[:, h * 128:(h + 1) * 128],
                                 op0=ALU.mult, op1=ALU.add)
            dump('kr', kr, kr[:], [128, 8, 128])
            dump('sin00', SinAcc[0][0], SinAcc[0][0][:], [128, 1024])
            dump('sin01', SinAcc[0][1], SinAcc[0][1][:], [128, 1024])
            for s in range(NSEG):
                for d in range(2):
                    if not dbg.get('sin_in'):
                        P.dma('sp', SinAcc[s][d], sin_d[s, d], SinAcc[s][d][:], R=[SinAcc[s][d]])
            conv_step(10 ** 6)
            P.barrier()

        catT = SB("catT", [128, 16, 1024], BF16)
        WB = [SB("WB%d" % i, [128, 16, 512], BF16) for i in range(2)]
        wbi = [0]

        def load_wb(src_cols):
            b = WB[wbi[0] % 2]
            wbi[0] += 1
            for apx, off, n in src_cols:
                P.dma('sp', b, b[:, :, off:off + n], apx.rearrange("(c p) n -> p c n", p=128), R=[W16], W=[b])
            return b

        pM = PS("pM", [128, 2048], F32)
        pX0 = PS("pX0", [128, 512], F32)
        pX1 = PS("pX1", [128, 512], F32)
        pX2 = PS("pX2", [128, 512], F32)
        pT = PS("pT", [128, 1024], BF16)

        for s in SEGS:
            with ExitStack() as s1:
                xT = SB("xT", [128, 16, 1024], BF16, s1)
                xH = SB("xH", [128, 16, 64], BF16, s1)
                rt = [SB("rtB%d" % i, [128, 8, 64], F32, s1) for i in range(4)]
                for c4 in range(4):
                    P.dma('pool', xT, xT[:, 4 * c4:4 * c4 + 4, :],
                          xTs[s].rearrange("(c p) t -> p c t", p=128)[:, 4 * c4:4 * c4 + 4, :], W=[xT])
                P.dma('pool', xH, xH[:], xTh[s].rearrange("(c p) t -> p c t", p=128), W=[xH])
                QT = SB("QT", [128, 8, 128], BF16, s1)
                QTF = SB("QTF", [128, 8, 128], BF16, s1)
                QTB = SB("QTB", [128, 8, 128], BF16, s1)
                KT = SB("KT", [128, 8, 128], BF16, s1)
                KF = SB("KF", [128, 8, 128], BF16, s1)
                KB = SB("KB", [128, 8, 128], BF16, s1)
                VB = SB("VB", [128, 8, 128], BF16, s1)
                GS = SB("GS", [128, 8, 128], F32, s1)
                qkA = SB("qkA", [128, 8, 2, 128], F32, s1)
                qkA2 = SB("qkA2", [128, 8, 2, 128], F32, s1)
                VB2 = SB("VB2", [128, 8, 128], BF16, s1)
                GS2 = SB("GS2", [128, 8, 128], F32, s1)
                qkB = SB("qkB", [128, 8, 2, 128], BF16, s1)
                Oall = SB("Oall", [128, 8, 128], F32, s1)
                RO = SB("RO", [128, 8, 128], BF16, s1)
                SbN = [SB("SbN%d" % n, [128, 128], BF16, s1) for n in range(NCH)]
                SfN = SB("SfN", [128, 128], BF16, s1)
                Sf = SB("Sf", [128, 128], F32, s1)
                Sb = SB("Sb", [128, 128], F32, s1)
                scm = SB("scm", [128, 128], BF16, s1)
                st1 = SB("st1", [128, 4, 8], F32, s1)
                hcs = SB("hcs", [128, 1026], F32, s1)
                us = SB("us", [128, 1026], F32, s1)
                bgs = SB("bgs", [128, 1024], F32, s1)
                cv = SB("cv", [128, 1024], F32, s1)

                hsets = [(qkA, VB, GS), (qkA2, VB2, GS2)]

                def stage1(h, st):
                    qkA, VB, GS = st
                    wb = load_wb([(w_in16[:, j * 1024 + h * 128: j * 1024 + (h + 1) * 128], j * 128, 128) for j in range(4)])
                    for n in range(NCH):
                        pp = pX0 if n % 2 == 0 else pX2
                        for c in range(16):
                            P.op('pe', 'matmul', R=[xT, wb], W=[pp], out=pp[:], lhsT=xT[:, c, 128 * n:128 * (n + 1)],
                                 rhs=wb[:, c, :], start=(c == 0), stop=(c == 15))
                        P.op('act', 'activation', R=[pp], W=[qkA], out=qkA[:, n].rearrange("p a d -> p (a d)"), in_=pp[:, 0:256], func=AF.Copy)
                        P.op('act', 'activation', R=[pp], W=[VB], out=VB[:, n, :], in_=pp[:, 256:384], func=AF.Copy)
                        P.op('act', 'activation', R=[pp], W=[GS], out=GS[:, n, :], in_=pp[:, 384:512], func=AF.Silu)

                def rest(h, st):
                    qkA, VB, GS = st
                    hs = slice(h * 128, (h + 1) * 128)
                    P.dma('sp', Sf, Sf[:], sin_d[s, 0][:, hs], W=[Sf])
                    P.dma('sp', Sb, Sb[:], sin_d[s, 1][:, hs], W=[Sb])
                    if HST < 1:
                        return
                    for r4 in range(2):
                        cs_ = slice(4 * r4, 4 * r4 + 4)
                        src = qkA[:, cs_]
                        dst = qkA[:, cs_]
                        cb = cso[:, 0, cs_, :].unsqueeze(2).to_broadcast([128, 4, 2, 64])
                        sb_ = cso[:, 1, cs_, :].unsqueeze(2).to_broadcast([128, 4, 2, 64])
                        x1 = src[:, :, :, 0:64]
                        x2 = src[:, :, :, 64:128]
                        tv = [t[:].rearrange("p (a b) d -> p a b d", b=2) for t in rt]
                        P.op('dve', 'tensor_tensor', R=[qkA, cso], W=[rt[0]], out=tv[0], in0=x1, in1=cb, op=ALU.mult)
                        P.op('dve', 'tensor_tensor', R=[qkA, cso], W=[rt[1]], out=tv[1], in0=x2, in1=sb_, op=ALU.mult)
                        P.op('pool', 'tensor_tensor', R=[qkA, cso], W=[rt[2]], out=tv[2], in0=x1, in1=sb_, op=ALU.mult)
                        P.op('pool', 'tensor_tensor', R=[qkA, cso], W=[rt[3]], out=tv[3], in0=x2, in1=cb, op=ALU.mult)
                        P.op('dve', 'tensor_tensor', R=[rt[0], rt[1]], W=[qkA], out=dst[:, :, :, 0:64], in0=tv[0], in1=tv[1], op=ALU.subtract)
                        P.op('pool', 'tensor_tensor', R=[rt[2], rt[3]], W=[qkA], out=dst[:, :, :, 64:128], in0=tv[2], in1=tv[3], op=ALU.add)
                    if HST < 2:
                        return
                    P.op('dve', 'tensor_copy', R=[qkA], W=[qkB], out=qkB[:], in_=qkA[:])
                    P.op('dve', 'tensor_scalar', R=[qkA, Z1], W=[KF], out=KF[:], in0=qkA[:, :, 1, :], scalar1=Z1[:, h:h + 1], scalar2=None, op0=ALU.mult)
                    P.op('pool', 'tensor_scalar', R=[qkA, Z1], W=[KB], out=KB[:], in0=qkA[:, :, 1, :], scalar1=Z1[:, 8 + h:9 + h], scalar2=None, op0=ALU.mult)
                    if HST < 3:
                        return
                    for r4 in range(2):
                        cs_ = slice(4 * r4, 4 * r4 + 4)
                        for n in range(4 * r4, 4 * r4 + 4):
                            for a_ in range(2):
                                o0 = (n % 4) * 256 + a_ * 128
                                P.op('pe', 'transpose', R=[qkB, idb], W=[pT], out=pT[:, o0:o0 + 128], in_=qkB[:, n, a_, :], identity=idb[:])
                        pT4 = pT[:].rearrange("p (n a d) -> p n a d", n=4, a=2)
                        P.op('act', 'activation', R=[pT], W=[QT], out=QT[:, cs_, :], in_=pT4[:, :, 0, :], func=AF.Copy)
                        P.op('dve', 'tensor_tensor', R=[pT, XiF], W=[QTF], out=QTF[:, cs_, :], in0=pT4[:, :, 0, :],
                             in1=XiF[:, h, :].unsqueeze(1).to_broadcast([128, 4, 128]), op=ALU.mult)
                        P.op('dve', 'tensor_tensor', R=[pT, XiB], W=[QTB], out=QTB[:, cs_, :], in0=pT4[:, :, 0, :],
                             in1=XiB[:, h, :].unsqueeze(1).to_broadcast([128, 4, 128]), op=ALU.mult)
                        P.op('act', 'activation', R=[pT], W=[KT], out=KT[:, cs_, :], in_=pT4[:, :, 1, :], func=AF.Copy)
                    if HST < 4:
                        return
                    for n in range(NCH - 1, -1, -1):
                        P.op('act', 'activation', R=[Sb], W=[SbN[n]], out=SbN[n][:], in_=Sb[:], func=AF.Copy)
                        if n > 0:
                            P.op('pe', 'matmul', R=[KB, VB], W=[pX2], out=pX2[:, 0:128], lhsT=KB[:, n, :], rhs=VB[:, n, :], start=True, stop=True)
                            P.op('dve', 'scalar_tensor_tensor', R=[Sb, G128, pX2], W=[Sb], out=Sb[:], in0=Sb[:], scalar=G128[:, 8 + h:9 + h],
                                 in1=pX2[:, 0:128], op0=ALU.mult, op1=ALU.add)
                    for n in range(NCH):
                        P.op('act', 'activation', R=[Sf], W=[SfN], out=SfN[:], in_=Sf[:], func=AF.Copy)
                        P.op('pe', 'matmul', R=[KT, QT], W=[pX1], out=pX1[:, 0:128], lhsT=KT[:, n, :], rhs=QT[:, n, :], start=True, stop=True)
                        P.op('dve', 'tensor_tensor', R=[pX1, DmT], W=[scm], out=scm[:], in0=pX1[:, 0:128], in1=DmT[:, h, :], op=ALU.mult)
                        P.op('pe', 'matmul', R=[scm, VB], W=[pX1], out=pX1[:, 128:256], lhsT=scm[:], rhs=VB[:, n, :], start=True, stop=False)
                        P.op('pe', 'matmul', R=[QTF, SfN], W=[pX1], out=pX1[:, 128:256], lhsT=QTF[:, n, :], rhs=SfN[:], start=False, stop=False)
                        P.op('pe', 'matmul', R=[QTB, SbN[n]], W=[pX1], out=pX1[:, 128:256], lhsT=QTB[:, n, :], rhs=SbN[n][:], start=False, stop=True)
                        if n < NCH - 1:
                            P.op('pe', 'matmul', R=[KF, VB], W=[pX2], out=pX2[:, 128:256], lhsT=KF[:, n, :], rhs=VB[:, n, :], start=True, stop=True)
                            P.op('dve', 'scalar_tensor_tensor', R=[Sf, G128, pX2], W=[Sf], out=Sf[:], in0=Sf[:], scalar=G128[:, h:h + 1],
                                 in1=pX2[:, 128:256], op0=ALU.mult, op1=ALU.add)
                        P.op('act', 'activation', R=[pX1], W=[Oall], out=Oall[:, n, :], in_=pX1[:, 128:256], func=AF.Copy)
                    if HST < 5:
                        return
                    sq = qkA[:].rearrange("p n a d -> p (n a d)")[:, 0:1024].rearrange("p (n d) -> p n d", n=8)
                    P.op('dve', 'tensor_reduce', R=[Oall], W=[st1], out=st1[:, 0, :], in_=Oall[:], axis=AX.X, op=ALU.add)
                    P.op('dve', 'tensor_scalar', R=[st1], W=[st1], out=st1[:, 1, :], in0=st1[:, 0, :], scalar1=-1.0 / 128, scalar2=None, op0=ALU.mult)
                    P.op('dve', 'tensor_tensor', R=[Oall, st1], W=[Oall], out=Oall[:], in0=Oall[:], in1=st1[:, 1, :].unsqueeze(2).to_broadcast([128, 8, 128]), op=ALU.add)
                    if HST < 5.2:
                        return
                    P.op('pool', 'tensor_tensor', R=[Oall], W=[qkA], out=sq, in0=Oall[:], in1=Oall[:], op=ALU.mult)
                    P.op('dve', 'tensor_reduce', R=[qkA], W=[st1], out=st1[:, 2, :], in_=sq, axis=AX.X, op=ALU.add)
                    P.op('dve', 'tensor_scalar', R=[st1], W=[st1], out=st1[:, 3, :], in0=st1[:, 2, :], scalar1=1.0 / 128, scalar2=GN_EPS, op0=ALU.mult, op1=ALU.add)
                    if HST < 5.3:
                        return
                    P.op('act', 'activation', R=[st1], W=[st1], out=st1[:, 3, :], in_=st1[:, 3, :], func=AF.Sqrt)
                    P.op('dve', 'reciprocal', R=[st1], W=[st1], out=st1[:, 3, :], in_=st1[:, 3, :])
                    if HST < 5.4:
                        return
                    P.op('pool', 'tensor_tensor', R=[GS, gn], W=[GS], out=GS[:], in0=GS[:], in1=gn[:, hs].unsqueeze(1).to_broadcast([128, 8, 128]), op=ALU.mult)
                    P.op('dve', 'tensor_tensor', R=[Oall, st1], W=[Oall], out=Oall[:], in0=Oall[:], in1=st1[:, 3, :].unsqueeze(2).to_broadcast([128, 8, 128]), op=ALU.mult)
                    P.op('dve', 'tensor_tensor', R=[Oall, GS], W=[RO], out=RO[:], in0=Oall[:], in1=GS[:], op=ALU.mult)
                    if HST < 5.5:
                        return
                    for n in range(NCH):
                        P.op('pe', 'transpose', R=[RO, idb], W=[pT], out=pT[:, n * 128:(n + 1) * 128], in_=RO[:, n, :], identity=idb[:])
                    P.op('act', 'activation', R=[pT], W=[catT], out=catT[:, h, :], in_=pT[:], func=AF.Copy)

                for i_, h_ in enumerate(HEADS):
                    if i_ == 0:
                        stage1(h_, hsets[0])
                    if i_ + 1 < len(HEADS):
                        stage1(HEADS[i_ + 1], hsets[(i_ + 1) % 2])
                    rest(h_, hsets[i_ % 2])
                pcs = [pX0, pX1]
                for gi in GROUPS:
                    wb = load_wb([(w_in16[:, 4096 + j * 1024 + gi * 128: 4096 + j * 1024 + (gi + 1) * 128], j * 128, 128) for j in range(3)])
                    for j in (0, 2, 1):
                        for si in range(2):
                            pc = pcs[si]
                            c0 = 512 * si
                            for c in range(16):
                                P.op('pe', 'matmul', R=[xT, wb], W=[pc], out=pc[:], lhsT=wb[:, c, j * 128:(j + 1) * 128],
                                     rhs=xT[:, c, c0:c0 + 512], start=(c == 0), stop=(c == 15))
                            if j == 0:
                                P.op('act', 'activation', R=[pc], W=[hcs], out=hcs[:, 1 + c0:1 + c0 + 512], in_=pc[:], func=AF.Copy)
                            elif j == 2:
                                P.op('dve', 'tensor_tensor', R=[pc, hcs], W=[us], out=us[:, 1 + c0:1 + c0 + 512], in0=pc[:], in1=hcs[:, 1 + c0:1 + c0 + 512], op=ALU.mult)
                            else:
                                P.op('act', 'activation', R=[pc], W=[bgs], out=bgs[:, c0:c0 + 512], in_=pc[:], func=AF.Copy)
                        if j != 1:
                            for c in range(16):
                                P.op('pe', 'matmul', R=[xH, wb], W=[pX2], out=pX2[:, 0:2], lhsT=wb[:, c, j * 128:(j + 1) * 128],
                                     rhs=xH[:, c, 0:2], start=(c == 0), stop=(c == 15))
                            if j == 0:
                                P.op('act', 'activation', R=[pX2], W=[hcs], out=hcs[:, 0:1], in_=pX2[:, 0:1], func=AF.Copy)
                                P.op('act', 'activation', R=[pX2], W=[hcs], out=hcs[:, 1025:1026], in_=pX2[:, 1:2], func=AF.Copy)
                            else:
                                P.op('dve', 'tensor_tensor', R=[pX2, hcs], W=[us], out=us[:, 0:1], in0=pX2[:, 0:1], in1=hcs[:, 0:1], op=ALU.mult)
                                P.op('dve', 'tensor_tensor', R=[pX2, hcs], W=[us], out=us[:, 1025:1026], in0=pX2[:, 1:2], in1=hcs[:, 1025:1026], op=ALU.mult)
                    P.op('dve', 'tensor_scalar', R=[us, cw], W=[cv], out=cv[:], in0=us[:, 0:1024], scalar1=cw[:, gi, 0:1], scalar2=None, op0=ALU.mult)
                    P.op('dve', 'scalar_tensor_tensor', R=[us, cw, cv], W=[cv], out=cv[:], in0=us[:, 1:1025], scalar=cw[:, gi, 1:2], in1=cv[:], op0=ALU.mult, op1=ALU.add)
                    P.op('dve', 'scalar_tensor_tensor', R=[us, cw, cv], W=[cv], out=cv[:], in0=us[:, 2:1026], scalar=cw[:, gi, 2:3], in1=cv[:], op0=ALU.mult, op1=ALU.add)
                    P.op('dve', 'tensor_tensor', R=[cv, bgs], W=[catT], out=catT[:, 8 + gi, :], in0=cv[:], in1=bgs[:], op=ALU.mult)
                dump('catT', catT, catT[:], [128, 16, 1024], BF16)
                P.barrier()

            with ExitStack() as s2:
                xt = SB("xt", [128, 2048], F32, s2)
                z = SB("z", [128, 2048], F32, s2)
                zq = SB("zq", [128, 2048], F32, s2)
                hh = SB("hh", [128, 2048], F32, s2)
                hb = SB("hb", [128, 2048], BF16, s2)
                hT = SB("hT", [128, 16, 128], BF16, s2)
                qp = SB("qp", [128, 2048], BF16, s2)
                qpT = hT
                vG = SB("vG", [128, 2048], F32, s2)
                vB = vG
                st2 = SB("st2", [128, 4], F32, s2)
                v16 = SB("v16", [128, 16, 16], F32, s2)
                ix16 = SB("ix16", [128, 16, 16], U32, s2)
                ixf = SB("ixf", [128, 16, 16], F32, s2)
                tmp = SB("tmp", [128, 256], F32, s2)
                best = SB("best", [128, 8, 16], F32, s2)
                pos = SB("pos", [128, 8, 16], U32, s2)
                pab = SB("pab", [128, 2, 8, 16], U32, s2)
                pabf = SB("pabf", [128, 2, 8, 16], F32, s2)
                isel = SB("isel", [128, 2, 8, 16], F32, s2)
                eidf = SB("eidf", [128, 128], F32, s2)
                gate = SB("gate", [128, 8, 16], F32, s2)
                gst = SB("gst", [128, 16], F32, s2)
                eidI = SB("eidI", [128, 128], I32, s2)
                actT = SB("actT", [128, 128], F32, s2)
                coefT = SB("coefT", [128, 128], F32, s2)
                idf = SB("idf", [128, 128], F32, s2)
                UVs = [SB("UV%d" % i, [128, 4096], BF16, s2) for i in range(NUV)]
                print("sbuf remaining (scope2)", nc.sbuf_bytes_remaining)
                actC = [Buf(actT.t) for _ in range(128)]
                coefC = [Buf(coefT.t) for _ in range(128)]
                dg = [SB("dg%d" % i, [128, 128], BF16, s2) for i in range(3)]
                P.op('dve', 'tensor_copy', R=[ct], W=[idf], out=idf[:], in_=ct[:, C_ID:C_ID + 128])
                cand = z
                oh = zq

                def layer_norm(src, dst, gi, extra_dst=None):
                    P.dma('sp', vG, vG[:], lnv[gi], W=[vG])
                    P.op('dve', 'reduce_sum', R=[src], W=[st2], out=st2[:, 0:1], in_=src[:], axis=AX.X)
                    P.op('dve', 'tensor_scalar', R=[st2], W=[st2], out=st2[:, 1:2], in0=st2[:, 0:1], scalar1=-1.0 / 2048, scalar2=None, op0=ALU.mult)
                    P.op('dve', 'tensor_scalar', R=[src, st2], W=[src], out=src[:], in0=src[:], scalar1=st2[:, 1:2], scalar2=None, op0=ALU.add)
                    P.op('pool', 'tensor_tensor', R=[src], W=[zq], out=zq[:], in0=src[:], in1=src[:], op=ALU.mult)
                    P.op('dve', 'reduce_sum', R=[zq], W=[st2], out=st2[:, 2:3], in_=zq[:], axis=AX.X)
                    P.op('dve', 'tensor_scalar', R=[st2], W=[st2], out=st2[:, 3:4], in0=st2[:, 2:3], scalar1=1.0 / 2048, scalar2=LN_EPS, op0=ALU.mult, op1=ALU.add)
                    P.op('act', 'activation', R=[st2], W=[st2], out=st2[:, 3:4], in_=st2[:, 3:4], func=AF.Sqrt)
                    P.op('dve', 'reciprocal', R=[st2], W=[st2], out=st2[:, 3:4], in_=st2[:, 3:4])
                    P.op('dve', 'scalar_tensor_tensor', R=[src, st2, vG], W=[dst], out=dst[:], in0=src[:], scalar=st2[:, 3:4], in1=vG[:], op0=ALU.mult, op1=ALU.mult)
                    P.dma('sp', vB, vB[:], lnv[gi + 1], W=[vB])
                    P.op('pool', 'tensor_tensor', R=[dst, vB], W=[dst], out=dst[:], in0=dst[:], in1=vB[:], op=ALU.add)

                if cat_in is not None:
                    P.dma('pool', catT, catT[:], cat_in, W=[catT])
                for n in TILES:
                    ts_ = slice(128 * n, 128 * (n + 1))
                    P.dma('sp', xt, xt[:], xtok[s, ts_, :], W=[xt])
                    for cb in range(4):
                        wb = load_wb([(w_out16[:, cb * 512:(cb + 1) * 512], 0, 512)])
                        for c in range(16):
                            P.op('pe', 'matmul', R=[catT, wb], W=[pM], out=pM[:, cb * 512:(cb + 1) * 512], lhsT=catT[:, c, ts_],
                                 rhs=wb[:, c, :], start=(c == 0), stop=(c == 15))
                    P.op('dve', 'scalar_tensor_tensor', R=[xt, pM], W=[z], out=z[:], in0=xt[:], scalar=ALPHA, in1=pM[:], op0=ALU.mult, op1=ALU.add)
                    layer_norm(z, hh, 0)
                    dump('h1', hh, hh[:], [128, 2048])
                    P.op('act', 'activation', R=[hh], W=[hb], out=hb[:], in_=hh[:], func=AF.Copy)
                    for c in range(16):
                        P.op('pe', 'transpose', R=[hb, idb], W=[pT], out=pT[:, (c % 8) * 128:(c % 8 + 1) * 128], in_=hb[:, c * 128:(c + 1) * 128], identity=idb[:])
                        if c % 8 == 7:
                            c0 = c - 7
                            P.op('act', 'activation', R=[pT], W=[hT], out=hT[:, c0:c0 + 8, :].rearrange("p c t -> p (c t)"), in_=pT[:], func=AF.Copy)
                    for cb in range(4):
                        wb = load_wb([(w_q16[:, cb * 512:(cb + 1) * 512], 0, 512)])
                        for c in range(16):
                            P.op('pe', 'matmul', R=[hT, wb], W=[pX0], out=pX0[:], lhsT=hT[:, c, :], rhs=wb[:, c, :], start=(c == 0), stop=(c == 15))
                        P.op('act', 'activation', R=[pX0], W=[qp], out=qp[:, cb * 512:(cb + 1) * 512], in_=pX0[:], func=AF.Copy)
                    for c in range(16):
                        P.op('pe', 'transpose', R=[qp, idb], W=[pT], out=pT[:, (c % 8) * 128:(c % 8 + 1) * 128], in_=qp[:, c * 128:(c + 1) * 128], identity=idb[:])
                        if c % 8 == 7:
                            c0 = c - 7
                            P.op('act', 'activation', R=[pT], W=[qpT], out=qpT[:, c0:c0 + 8, :].rearrange("p c t -> p (c t)"), in_=pT[:], func=AF.Copy)
                    for mm in range(16):
                        P.op('pe', 'matmul', R=[qpT, kT16], W=[pM], out=pM[:, mm * 128:(mm + 1) * 128], lhsT=qpT[:, mm, :], rhs=kT16[:, mm, :], start=True, stop=True)
                    sc = xt
                    P.op('act', 'activation', R=[pM], W=[sc], out=sc[:], in_=pM[:], func=AF.Copy)
                    dump('sc', sc, sc[:], [128, 2048])
                    for mm in range(16):
                        srow = sc[:, mm * 128:(mm + 1) * 128]
                        P.op('dve', 'max', R=[sc], W=[v16], out=v16[:, mm, 0:8], in_=srow)
                        P.op('dve', 'max_index', R=[sc, v16], W=[ix16], out=ix16[:, mm, 0:8], in_max=v16[:, mm, 0:8], in_values=srow)
                        P.op('dve', 'match_replace', R=[sc, v16], W=[tmp], out=tmp[:, 0:128], in_to_replace=v16[:, mm, 0:8], in_values=srow, imm_value=-1e30)
                        P.op('dve', 'max', R=[tmp], W=[v16], out=v16[:, mm, 8:16], in_=tmp[:, 0:128])
                        P.op('dve', 'max_index', R=[tmp, v16], W=[ix16], out=ix16[:, mm, 8:16], in_max=v16[:, mm, 8:16], in_values=tmp[:, 0:128])
                    P.op('dve', 'tensor_copy', R=[ix16], W=[ixf], out=ixf[:], in_=ix16[:])
                    v4 = v16[:].rearrange("p (h two) k -> p h two k", two=2)
                    i4 = ixf[:].rearrange("p (h two) k -> p h two k", two=2)
                    c4 = cand[:].rearrange("p (h a b) -> p h a b", h=8, a=16)
                    P.op('dve', 'tensor_tensor', R=[v16], W=[cand], out=c4, in0=v4[:, :, 0, :].unsqueeze(3).to_broadcast([128, 8, 16, 16]),
                         in1=v4[:, :, 1, :].unsqueeze(2).to_broadcast([128, 8, 16, 16]), op=ALU.add)
                    for h in range(8):
                        crow = cand[:, h * 256:(h + 1) * 256]
                        P.op('dve', 'max', R=[cand], W=[best], out=best[:, h, 0:8], in_=crow)
                        P.op('dve', 'max_index', R=[cand, best], W=[pos], out=pos[:, h, 0:8], in_max=best[:, h, 0:8], in_values=crow)
                        P.op('dve', 'match_replace', R=[cand, best], W=[tmp], out=tmp[:], in_to_replace=best[:, h, 0:8], in_values=crow, imm_value=-1e30)
                        P.op('dve', 'max', R=[tmp], W=[best], out=best[:, h, 8:16], in_=tmp[:])
                        P.op('dve', 'max_index', R=[tmp, best], W=[pos], out=pos[:, h, 8:16], in_max=best[:, h, 8:16], in_values=tmp[:])
                    P.op('dve', 'tensor_single_scalar', R=[pos], W=[pab], out=pab[:, 0], in_=pos[:], scalar=4, op=ALU.logical_shift_right)
                    P.op('dve', 'tensor_single_scalar', R=[pos], W=[pab], out=pab[:, 1], in_=pos[:], scalar=15, op=ALU.bitwise_and)
                    P.op('dve', 'tensor_copy', R=[pab], W=[pabf], out=pabf[:], in_=pab[:])
                    o4 = oh[:].rearrange("p (h k a) -> p h k a", h=8, k=16)
                    iob = ct[:, C_IOTA:C_IOTA + 16].unsqueeze(1).unsqueeze(1).to_broadcast([128, 8, 16, 16])
                    for w_ in range(2):
                        P.op('dve', 'tensor_tensor', R=[pabf, ct], W=[oh], out=o4, in0=pabf[:, w_].unsqueeze(3).to_broadcast([128, 8, 16, 16]), in1=iob, op=ALU.is_equal)
                        P.op('dve', 'tensor_tensor', R=[oh, ixf], W=[oh], out=o4, in0=o4, in1=i4[:, :, w_, :].unsqueeze(2).to_broadcast([128, 8, 16, 16]), op=ALU.mult)
                        P.op('dve', 'tensor_reduce', R=[oh], W=[isel], out=isel[:, w_], in_=o4, axis=AX.X, op=ALU.add)
                    P.op('dve', 'scalar_tensor_tensor', R=[isel], W=[eidf], out=eidf[:].rearrange("p (h k) -> p h k", h=8), in0=isel[:, 0], scalar=128.0, in1=isel[:, 1], op0=ALU.mult, op1=ALU.add)
                    P.op('dve', 'tensor_tensor', R=[best], W=[gate], out=gate[:], in0=best[:], in1=best[:, :, 0:1].to_broadcast([128, 8, 16]), op=ALU.subtract)
                    P.op('act', 'activation', R=[gate], W=[gate], out=gate[:], in_=gate[:], func=AF.Exp)
                    P.op('dve', 'tensor_reduce', R=[gate], W=[gst], out=gst[:, 0:8], in_=gate[:], axis=AX.X, op=ALU.add)
                    P.op('dve', 'reciprocal', R=[gst], W=[gst], out=gst[:, 8:16], in_=gst[:, 0:8])
                    P.op('dve', 'tensor_tensor', R=[gate, gst], W=[gate], out=gate[:], in0=gate[:], in1=gst[:, 8:16].unsqueeze(2).to_broadcast([128, 8, 16]), op=ALU.mult)
                    dump('eidf', eidf, eidf[:], [128, 128])
                    dump('gate', gate, gate[:], [128, 8, 16])
                    P.op('dve', 'tensor_copy', R=[eidf], W=[eidI], out=eidI[:], in_=eidf[:])
                    P.op('pool', 'memset', W=actC, ap=actT[:], constant=0.0)
                    gate2 = gate[:].rearrange("p h k -> p (h k)")

                    def gUV(k):
                        ub = UVs[k % NUV]
                        P.dma('pool', ub, ub[:], uv16, R=[eidI, UV16], W=[ub], name='indirect_dma_start', out_offset=None,
                              in_offset=bass.IndirectOffsetOnAxis(ap=eidI[:, k:k + 1], axis=0))

                    def vstep(k):
                        vb_ = UVs[k % NUV]
                        dgb = dg[k % len(dg)]
                        P.op('act', 'activation', R=[coefC[k], gate], W=[coefC[k]], out=coefT[:, k:k + 1], in_=coefT[:, k:k + 1], func=AF.Copy, scale=gate2[:, k:k + 1])
                        P.op('act', 'activation', R=[idf, coefC[k]], W=[dgb], out=dgb[:], in_=idf[:], func=AF.Copy, scale=coefT[:, k:k + 1])
                        for q4 in range(4):
                            P.op('pe', 'matmul', R=[dgb, vb_], W=[pM], out=pM[:, q4 * 512:(q4 + 1) * 512], lhsT=dgb[:], rhs=vb_[:, 2048 + q4 * 512:2048 + (q4 + 1) * 512],
                                 start=(k == 0), stop=(k == NTOK - 1))

                    PF = NUV - 2
                    for k in range(min(PF, NTOK)):
                        gUV(k)
                    for k in range(NTOK):
                        if k + PF < NTOK:
                            gUV(k + PF)
                        ub = UVs[k % NUV]
                        P.op('dve', 'scalar_tensor_tensor', R=[ub, hb], W=[qp, actC[k]], out=qp[:], in0=ub[:, 0:2048], scalar=1.0, in1=hb[:], op0=ALU.mult, op1=ALU.mult,
                             accum_out=actT[:, k:k + 1])
                        P.op('act', 'activation', R=[actC[k]], W=[coefC[k]], out=coefT[:, k:k + 1], in_=actT[:, k:k + 1], func=AF.Gelu)
                        if k >= 1:
                            vstep(k - 1)
                    vstep(NTOK - 1)
                    if 'actT' in dbg.get('dump', []):
                        P.op('dve', 'tensor_copy', R=actC, W=[actT], out=actT[:], in_=actT[:])
                    dump('actT', actT, actT[:], [128, 128])
                    P.op('dve', 'scalar_tensor_tensor', R=[hh, pM], W=[z], out=z[:], in0=hh[:], scalar=ALPHA, in1=pM[:], op0=ALU.mult, op1=ALU.add)
                    dump('z2', z, z[:], [128, 2048])
                    layer_norm(z, hh, 2)
                    P.dma('pool', hh, y[s, ts_, :], hh[:], R=[hh])
                P.barrier()
        P.barrier()

        with nc.Block() as block:
            @block.tensor
            def _(E):
                P.replay('pe', E)

            @block.vector
            def _(E):
                P.replay('dve', E)

            @block.scalar
            def _(E):
                P.replay('act', E)

            @block.gpsimd
            def _(E):
                P.replay('pool', E)

            @block.sync
            def _(E):
                P.replay('sp', E)
    return nc


def _host_tables(c):
    j = np.arange(128, dtype=np.float64)[:, None]
    i = np.arange(128, dtype=np.float64)[None, :]
    t = np.zeros((128, NT), np.float32)
    t[:, C_EPF:C_EPF + 128] = np.maximum(i - j, 0)
    t[:, C_MPF:C_MPF + 128] = (i >= j)
    t[:, C_EPB:C_EPB + 128] = np.maximum(j - i, 0)
    t[:, C_MPB:C_MPB + 128] = (j >= i)
    t[:, C_EXF:C_EXF + 128] = i + 1
    t[:, C_EXB:C_EXB + 128] = 128 - i
    t[:, C_EZF1] = 127 - j[:, 0]
    t[:, C_EZB1] = j[:, 0]
    m = np.arange(8)[None, :]
    t[:, C_EZFA:C_EZFA + 8] = 1023 - (128 * m + j)
    t[:, C_EZBA:C_EZBA + 8] = 128 * m + j
    rl = np.arange(8)
    r = rl + (rl >= c)
    valid = (rl < 7)
    t[:, C_EWF:C_EWF + 8] = (1024 * np.maximum(c - 1 - r, 0) * valid)[None, :]
    t[:, C_MWF:C_MWF + 8] = ((r < c) & valid)[None, :]
    t[:, C_EWB:C_EWB + 8] = (1024 * np.maximum(r - c - 1, 0) * valid)[None, :]
    t[:, C_MWB:C_MWB + 8] = ((r > c) & valid)[None, :]
    t[:, C_IOTA:C_IOTA + 16] = np.arange(16)[None, :]
    t[:, C_ID:C_ID + 128] = np.eye(128)
    return t


_NC_CACHE = {}


def _prep(x_prompt, x_sample, w_in, ret_log_rate_fwd, ret_log_rate_bwd, ret_gn_g, conv_w, w_out,
          ln1_g, ln1_b, peer_w_query, peer_keys_1, peer_keys_2, peer_u, peer_v, ln2_g, ln2_b, cores=range(8)):
    f = np.float32
    xs = [np.asarray(x_prompt[0], f), np.asarray(x_sample[0], f), np.asarray(x_sample[1], f)]
    xTfull = [np.ascontiguousarray(x.T) for x in xs]
    inv = (np.float32(10000.0) ** (-np.arange(0, 128, 2, dtype=np.float32) / np.float32(128))).astype(np.float32)
    ang = (np.arange(8192, dtype=np.float32)[:, None] * inv[None, :]).astype(np.float32)
    cosT = np.cos(ang.astype(np.float64)).astype(f)
    sinT = np.sin(ang.astype(np.float64)).astype(f)
    cs = np.stack([cosT, sinT])
    csall = np.ascontiguousarray(cs.reshape(2, 64, 128, 64).transpose(2, 0, 1, 3))
    keys = np.stack([np.asarray(peer_keys_1[0], f), np.asarray(peer_keys_2[0], f)], axis=1)
    keysT = np.ascontiguousarray(keys.reshape(16, 128, 128).transpose(2, 0, 1))
    rep = lambda v: np.ascontiguousarray(np.broadcast_to(np.asarray(v, f).reshape(1, -1), (128, np.asarray(v).size)))
    lnv = np.stack([rep(ln1_g[0]), rep(ln1_b[0]), rep(ln2_g[0]), rep(ln2_b[0])])
    gng = rep(ret_gn_g[0])
    cwT = np.ascontiguousarray(np.asarray(conv_w[0], f).T.reshape(8, 128, 3).transpose(1, 0, 2))
    rates = np.concatenate([np.asarray(ret_log_rate_fwd[0], f), np.asarray(ret_log_rate_bwd[0], f)])
    w_in0 = np.ascontiguousarray(np.asarray(w_in[0], f))
    w_out0 = np.ascontiguousarray(np.asarray(w_out[0], f))
    w_q0 = np.ascontiguousarray(np.asarray(peer_w_query[0], f))
    pu0 = np.ascontiguousarray(np.asarray(peer_u[0], f))
    pv0 = np.ascontiguousarray(np.asarray(peer_v[0], f))
    in_maps = []
    for c in cores:
        lo = 1024 * c
        xtok = np.ascontiguousarray(np.stack([x[lo:lo + 1024] for x in xs]))
        xTs = np.zeros((3, 2048, 1024), f)
        xTh = np.zeros((3, 2048, 64), f)
        for s in range(3):
            xTs[s] = xs[s][lo:lo + 1024].T
            if lo - 1 >= 0:
                xTh[s, :, 0] = xs[s][lo - 1]
            if lo + 1024 < 8192:
                xTh[s, :, 1] = xs[s][lo + 1024]
        xTall = np.ascontiguousarray(np.stack([np.concatenate([xt_[:, :lo], xt_[:, lo + 1024:]], axis=1) for xt_ in xTfull]))
        oth = [g_ for g_ in range(64) if not (8 * c <= g_ < 8 * c + 8)]
        csoth = np.ascontiguousarray(csall[:, :, oth, :])
        ct = _host_tables(c)
        ct[:, C_RATE:C_RATE + 16] = rates[None, :]
        csown = np.ascontiguousarray(csall[:, :, 8 * c:8 * c + 8, :])
        in_maps.append(dict(xtok=xtok, xTs=xTs, xTh=xTh, xTall=xTall, w_in=w_in0, w_out=w_out0, w_q=w_q0, keysT=keysT,
                            pu=pu0, pv=pv0, ctab=ct, csall=csoth, csown=csown, gng=gng, cwT=cwT, lnv=lnv))
    return in_maps


def kernel(**inputs):
    f = np.float32
    in_maps = _prep(**inputs)
    if 'nc' not in _NC_CACHE:
        _NC_CACHE['nc'] = build()
    res = run_bass_kernel_spmd(_NC_CACHE['nc'], in_maps, core_ids=list(range(8)))
    ys = [np.zeros((8192, 2048), f) for _ in range(3)]
    for c in range(8):
        yc = res.results[c]["y"]
        for s in range(3):
            ys[s][1024 * c:1024 * (c + 1)] = yc[s]
    return (ys[0][None], np.stack([ys[1], ys[2]]))
```
